# Optimizing a Trainium2 kernel written in Bass

```python
import jax, jax.numpy as jnp
from jax import lax
import numpy as np

D_MODEL = 1024
BATCH = 16
SEQ = 2048
DEPTH = 1
DEC_BATCH = 32
DEC_SEQ = 16
PAST_LEN = 1024

CHUNK = 64
N_HEADS = 8
HEAD_DIM = 64
ATTN_WIDTH = N_HEADS * HEAD_DIM
POOL_WINDOWS = (2, 4, 8, 16)
N_POOL_GROUPS = len(POOL_WINDOWS)
POOL_WIDTH = D_MODEL // 2
POOL_GROUP = POOL_WIDTH // N_POOL_GROUPS
POOL_STATE = max(POOL_WINDOWS) - 1
D_FF = 2816
CONV_WIDTH = 3
Q_BLOCK = 128
EPS = 1e-6
IN_WIDTH = 3 * ATTN_WIDTH + POOL_WIDTH + 2 * D_MODEL

kernel_name = "stick_breaking_pool_hybrid_stream_step"


def rmsnorm(x, g):
    x32 = x.astype(jnp.float32)
    y = x32 * lax.rsqrt(jnp.mean(x32 * x32, axis=-1, keepdims=True) + EPS) * g.astype(jnp.float32)
    return y.astype(x.dtype)


def stick_breaking(q, k, v, q_pos, k_pos):
    z = jnp.einsum('bqhd,bkhd->bhqk', q.astype(jnp.float32), k.astype(jnp.float32)) * (HEAD_DIM ** -0.5)
    mask = (k_pos[None, :] < q_pos[:, None])[None, None]
    log_fail = jnp.where(mask, jax.nn.log_sigmoid(-z), 0.0)
    log_between = lax.cumsum(log_fail, axis=3, reverse=True) - log_fail
    a = jnp.where(mask, jnp.exp(jax.nn.log_sigmoid(z) + log_between), 0.0)
    return jnp.einsum('bhqk,bkhd->bqhd', a, v.astype(jnp.float32)).astype(v.dtype)


def stick_breaking_prompt(q, k, v):
    T = q.shape[1]
    pos = jnp.arange(T)
    outs = []
    for i in range(T // Q_BLOCK):
        lo, hi = i * Q_BLOCK, (i + 1) * Q_BLOCK
        outs.append(stick_breaking(q[:, lo:hi], k[:, :hi], v[:, :hi], pos[lo:hi], pos[:hi]))
    return jnp.concatenate(outs, axis=1)


def multi_scale_pool(u, left, pos0):
    T = u.shape[1]
    ext_raw = jnp.concatenate([left, u], axis=1)
    cs = jnp.cumsum(ext_raw.astype(jnp.float32), axis=1)
    cs = jnp.concatenate([jnp.zeros_like(cs[:, :1]), cs], axis=1)
    pos = pos0 + jnp.arange(T)
    end = POOL_STATE + 1
    groups = []
    for g, w in enumerate(POOL_WINDOWS):
        sl = slice(g * POOL_GROUP, (g + 1) * POOL_GROUP)
        win = cs[:, end:end + T, sl] - cs[:, end - w:end - w + T, sl]
        cnt = jnp.minimum(pos + 1, w).astype(jnp.float32)[None, :, None]
        groups.append(win / cnt)
    mean = jnp.concatenate(groups, axis=-1)
    return mean - u.astype(jnp.float32), ext_raw[:, -POOL_STATE:]


def causal_dwconv(h, left, w, b):
    T = h.shape[1]
    ext = jnp.concatenate([left, h], axis=1)
    out = b.astype(jnp.float32)
    for j in range(CONV_WIDTH):
        out = out + ext[:, j:j + T].astype(jnp.float32) * w[j].astype(jnp.float32)
    return out.astype(h.dtype), ext[:, -(CONV_WIDTH - 1):]


def trunk_layer(x, pos0, kv_past, pool_left, conv_left, norm_mix, w_in, w_a, w_b,
                pool_w, pool_scale, w_o, norm_ffn, w_up, conv_w, conv_b, w_down):
    B, T, _ = x.shape
    h = rmsnorm(x, norm_mix)
    proj = h @ w_in
    cuts = [ATTN_WIDTH, 2 * ATTN_WIDTH, 3 * ATTN_WIDTH,
            3 * ATTN_WIDTH + POOL_WIDTH, 3 * ATTN_WIDTH + POOL_WIDTH + D_MODEL]
    q, k, v, u, g_a, g_b = jnp.split(proj, cuts, axis=-1)
    q = q.reshape(B, T, N_HEADS, HEAD_DIM)
    k = k.reshape(B, T, N_HEADS, HEAD_DIM)
    v = v.reshape(B, T, N_HEADS, HEAD_DIM)

    if kv_past is None:
        attn = stick_breaking_prompt(q, k, v)
    else:
        ck, cv = kv_past
        past = ck.shape[1]
        k_all = jnp.concatenate([ck, k], axis=1)
        v_all = jnp.concatenate([cv, v], axis=1)
        attn = stick_breaking(q, k_all, v_all, past + jnp.arange(T), jnp.arange(past + T))
    branch_a = attn.reshape(B, T, ATTN_WIDTH) @ w_a

    pooled, pool_new = multi_scale_pool(u, pool_left, pos0)
    pmix = jnp.einsum('btgc,gcd->btgd', pooled.reshape(B, T, N_POOL_GROUPS, POOL_GROUP),
                      pool_w.astype(jnp.float32)).reshape(B, T, POOL_WIDTH)
    pmix = pmix * pool_scale.astype(jnp.float32)
    branch_b = pmix.astype(x.dtype) @ w_b

    merged = jax.nn.sigmoid(g_a) * branch_a + jax.nn.sigmoid(g_b) * branch_b
    x = x + merged @ w_o

    h2 = rmsnorm(x, norm_ffn)
    up = h2 @ w_up
    c, conv_new = causal_dwconv(up, conv_left, conv_w, conv_b)
    gate, val = jnp.split(c, [D_FF], axis=-1)
    x = x + (jax.nn.gelu(gate) * val) @ w_down
    return x, k, v, pool_new, conv_new


def setup_inputs(seed: int = 0) -> dict:
    key = jax.random.key(seed)
    ks = jax.random.split(key, 24)
    f32 = jnp.float32
    nrm = lambda k, shape, scale: jax.random.normal(k, shape, f32) * scale
    return {
        "x_prompt": nrm(ks[0], (BATCH, SEQ, D_MODEL), 1.0),
        "x_sample": nrm(ks[1], (DEC_BATCH, DEC_SEQ, D_MODEL), 1.0),
        "cache_k": nrm(ks[2], (DEPTH, DEC_BATCH, PAST_LEN, N_HEADS, HEAD_DIM), 1.0),
        "cache_v": nrm(ks[3], (DEPTH, DEC_BATCH, PAST_LEN, N_HEADS, HEAD_DIM), 1.0),
        "state_pool": nrm(ks[4], (DEPTH, DEC_BATCH, POOL_STATE, POOL_WIDTH), 1.0),
        "state_conv": nrm(ks[5], (DEPTH, DEC_BATCH, CONV_WIDTH - 1, 2 * D_FF), 1.0),
        "norm_mix": 1.0 + nrm(ks[6], (DEPTH, D_MODEL), 0.02),
        "w_in": nrm(ks[7], (DEPTH, D_MODEL, IN_WIDTH), D_MODEL ** -0.5),
        "w_a": nrm(ks[8], (DEPTH, ATTN_WIDTH, D_MODEL), ATTN_WIDTH ** -0.5),
        "w_b": nrm(ks[9], (DEPTH, POOL_WIDTH, D_MODEL), POOL_WIDTH ** -0.5),
        "pool_w": nrm(ks[10], (DEPTH, N_POOL_GROUPS, POOL_GROUP, POOL_GROUP), POOL_GROUP ** -0.5),
        "pool_scale": 1.0 + nrm(ks[11], (DEPTH, POOL_WIDTH), 0.02),
        "w_o": nrm(ks[12], (DEPTH, D_MODEL, D_MODEL), D_MODEL ** -0.5),
        "norm_ffn": 1.0 + nrm(ks[13], (DEPTH, D_MODEL), 0.02),
        "w_up": nrm(ks[14], (DEPTH, D_MODEL, 2 * D_FF), D_MODEL ** -0.5),
        "conv_w": nrm(ks[15], (DEPTH, CONV_WIDTH, 2 * D_FF), CONV_WIDTH ** -0.5),
        "conv_b": nrm(ks[16], (DEPTH, 2 * D_FF), 0.01),
        "w_down": nrm(ks[17], (DEPTH, D_FF, D_MODEL), D_FF ** -0.5),
        "norm_final": 1.0 + nrm(ks[18], (D_MODEL,), 0.02),
    }


def reference(x_prompt, x_sample, cache_k, cache_v, state_pool, state_conv,
              norm_mix, w_in, w_a, w_b, pool_w, pool_scale, w_o, norm_ffn,
              w_up, conv_w, conv_b, w_down, norm_final):
    xp, xs = x_prompt, x_sample
    kp, vp, pp, cp = [], [], [], []
    ksl, vsl, psl, csl = [], [], [], []
    for l in range(DEPTH):
        weights = (norm_mix[l], w_in[l], w_a[l], w_b[l], pool_w[l], pool_scale[l], w_o[l],
                   norm_ffn[l], w_up[l], conv_w[l], conv_b[l], w_down[l])
        pool_zero = jnp.zeros((xp.shape[0], POOL_STATE, POOL_WIDTH), xp.dtype)
        conv_zero = jnp.zeros((xp.shape[0], CONV_WIDTH - 1, 2 * D_FF), xp.dtype)
        xp, k_new, v_new, pool_new, conv_new = trunk_layer(
            xp, 0, None, pool_zero, conv_zero, *weights)
        kp.append(k_new); vp.append(v_new); pp.append(pool_new); cp.append(conv_new)
        xs, k_new, v_new, pool_new, conv_new = trunk_layer(
            xs, cache_k.shape[2], (cache_k[l], cache_v[l]), state_pool[l], state_conv[l], *weights)
        ksl.append(k_new); vsl.append(v_new); psl.append(pool_new); csl.append(conv_new)
    y_prompt = rmsnorm(xp, norm_final)
    y_sample = rmsnorm(xs, norm_final)
    return (y_prompt, y_sample,
            jnp.stack(kp), jnp.stack(vp), jnp.stack(pp), jnp.stack(cp),
            jnp.stack(ksl), jnp.stack(vsl), jnp.stack(psl), jnp.stack(csl))
```

```python
import os
import contextlib
import numpy as np
import concourse.bass as bass
import concourse.mybir as mybir
from concourse.bass_utils import run_bass_kernel_spmd

F32 = mybir.dt.float32
BF16 = mybir.dt.bfloat16
AF = mybir.ActivationFunctionType
ALU = mybir.AluOpType

ENGS = ("pe", "act", "dve", "pool", "sp")
NSLOT = 6
NCELL = 200
NCORES = 8


class Op:
    __slots__ = ("eng", "fn", "reads", "writes", "dma_key", "idx", "deps", "signaled", "sig_val", "waits")

    def __init__(self, eng, fn, reads, writes, dma_key):
        self.eng = eng
        self.fn = fn
        self.reads = reads
        self.writes = writes
        self.dma_key = dma_key
        self.deps = None
        self.signaled = False
        self.sig_val = None
        self.waits = None


class Prog:
    def __init__(self, same_engine_sync=True):
        self.ops = []
        self.deferred = None
        self.same_engine_sync = same_engine_sync

    def op(self, eng, fn, reads=(), writes=(), dma_key=None):
        psr = [r for r in reads if isinstance(r, tuple) and r and r[0] == "ps"]
        if psr:
            reads = [r for r in reads if not (isinstance(r, tuple) and r and r[0] == "ps")]
            writes = list(writes) + [r for r in psr if r not in writes]
        o = Op(eng, fn, tuple(reads), tuple(writes), dma_key)
        if self.deferred is not None:
            self.deferred.append(o)
        else:
            self.ops.append(o)
        return o

    def begin_defer(self):
        self.deferred = []

    def end_defer(self):
        d, self.deferred = self.deferred, None
        return d

    def push(self, o):
        self.ops.append(o)

    def dma(self, eng, out, in_, reads=(), writes=(), key=None):
        return self.op(eng, lambda e: e.dma_start(out=out, in_=in_), reads, writes, dma_key=key)

    def need_sync(self, dop, o):
        if dop.dma_key is not None:
            return True
        if dop.eng != o.eng:
            return True
        if o.dma_key is not None:
            return True
        if dop.eng in ("pe", "sp"):
            return False
        return self.same_engine_sync

    def analyze(self):
        for i, o in enumerate(self.ops):
            o.idx = i
        last_w = {}
        readers = {}
        for o in self.ops:
            deps = set()
            war = set()
            for r in o.reads:
                w = last_w.get(r)
                if w is not None:
                    deps.add(w)
            for w_ in o.writes:
                w = last_w.get(w_)
                if w is not None:
                    deps.add(w)
                for rd in readers.get(w_, ()):
                    war.add(rd)
            for d in war:
                deps.add(d)
            deps.discard(o.idx)
            o.deps = deps
            for r in o.reads:
                readers.setdefault(r, []).append(o.idx)
            for w_ in o.writes:
                last_w[w_] = o.idx
                readers[w_] = []
        for o in self.ops:
            for d in o.deps:
                dop = self.ops[d]
                if self.need_sync(dop, o):
                    dop.signaled = True
        cnt = {e: 0 for e in ENGS}
        dcnt = {}
        for o in self.ops:
            if o.dma_key is not None:
                dcnt[o.dma_key] = dcnt.get(o.dma_key, 0) + 16
                o.sig_val = dcnt[o.dma_key]
            elif o.signaled:
                cnt[o.eng] += 1
                o.sig_val = cnt[o.eng]
        self.dma_keys = list(dcnt.keys())
        waited = {e: {} for e in ENGS}
        for o in self.ops:
            need = {}
            for d in o.deps:
                dop = self.ops[d]
                if not self.need_sync(dop, o):
                    continue
                sk = ("dma", dop.dma_key) if dop.dma_key is not None else ("eng", dop.eng)
                if dop.sig_val > need.get(sk, 0):
                    need[sk] = dop.sig_val
            ws = []
            wd = waited[o.eng]
            for sk, v in need.items():
                if wd.get(sk, 0) >= v:
                    continue
                wd[sk] = v
                ws.append((sk, v))
            o.waits = ws
        self.final_dma = dict(dcnt)

    def emit(self, block, sems):
        per = {e: [] for e in ENGS}
        for o in self.ops:
            per[o.eng].append(o)
        final_dma = self.final_dma

        def run(eh, ename):
            for o in per[ename]:
                for sk, v in o.waits:
                    eh.wait_ge(sems[sk], v)
                ins = o.fn(eh)
                if o.dma_key is not None:
                    ins.then_inc(sems[("dma", o.dma_key)], 16)
                elif o.signaled:
                    ins.then_inc(sems[("eng", ename)], 1)
            if ename == "sp":
                for k, v in final_dma.items():
                    eh.wait_ge(sems[("dma", k)], v)

        block.tensor(lambda e: run(e, "pe"))
        block.scalar(lambda e: run(e, "act"))
        block.vector(lambda e: run(e, "dve"))
        block.gpsimd(lambda e: run(e, "pool"))
        block.sync(lambda e: run(e, "sp"))


def build_nc(tiles=None):
    nc = bass.Bass("TRN2", target_bir_lowering=False)
    D = {}

    def din(name, shape):
        D[name] = nc.dram_tensor(name, list(shape), F32, kind="ExternalInput").ap()

    def dout(name, shape):
        D[name] = nc.dram_tensor(name, list(shape), F32, kind="ExternalOutput").ap()

    din("xp", (2, 2048, 1024)); din("xs", (64, 1024)); din("ck", (4, 1024, 512)); din("cv", (4, 1024, 512))
    din("spool", (4, 15, 512)); din("sconv", (4, 2, 5632))
    din("norm_mix", (1024,)); din("w_in", (1024, 4096)); din("w_a", (512, 1024)); din("w_b", (512, 1024))
    din("pool_w", (4, 128, 128)); din("pool_scale", (512,)); din("w_o", (1024, 1024)); din("norm_ffn", (1024,))
    din("w_up", (1024, 5632)); din("conv_w", (3, 5632)); din("conv_b", (5632,)); din("w_down", (2816, 1024))
    din("norm_final", (1024,))
    dout("yp", (2, 2048, 1024)); dout("ys", (64, 1024)); dout("kp", (2, 2048, 512)); dout("vp", (2, 2048, 512))
    dout("pp", (2, 15, 512)); dout("cp", (2, 2, 5632)); dout("ks", (64, 512)); dout("vs", (64, 512))
    dout("ps", (4, 15, 512)); dout("cs", (4, 2, 5632))

    P = Prog()
    with contextlib.ExitStack() as es:
        def sb(name, shape, dt):
            return es.enter_context(nc.sbuf_tensor(name, list(shape), dt))

        ring = sb("ring", (128, NSLOT, 4096), BF16)
        xt2 = sb("xt", (128, 2, 4, 1024), F32)
        hT = sb("hT", (128, 8, 512), BF16)
        hb = sb("hb", (128, 1024), BF16)
        KT = sb("KT", (128, 4, 2304), BF16)
        Vb = sb("Vb", (128, 18, 512), BF16)
        attnT = sb("attnT", (128, 4, 512), BF16)
        uctx = sb("uctx", (128, 4, 15), F32)
        ctx = sb("ctx", (128, 4, 2, 44), F32)
        gmix = sb("gmix", (128, 1024), F32)
        gffn = sb("gffn", (128, 1024), F32)
        gfin = sb("gfin", (128, 1024), F32)
        cbf = sb("cbf", (128, 576), BF16)
        identf = sb("identf", (128, 128), F32)
        cpar = sb("cpar", (128, 180), F32)
        cparst = sb("cparst", (44, 4, 128), F32)
        pscst = sb("pscst", (4, 128), F32)
        invc = sb("invc", (128, 4, 15), F32)
        poolw = sb("poolw", (128, 4, 128), BF16)
        small = sb("small", (128, 16), F32)
        QTa = sb("QTa", (128, 4, 512), BF16)
        QTb = sb("QTb", (128, 4, 512), BF16)
        scr = sb("scr", (128, 128 * NCELL), BF16)
        banks = [es.enter_context(nc.psum_tensor(f"bank{i}", [128, 512], F32)) for i in range(8)]
        bankr = [("ps", i) for i in range(8)]

        ident = cbf[:, 0:128]
        negU = cbf[:, 128:256]
        negL = cbf[:, 256:384]
        maskD = cbf[:, 384:512]
        maskS4 = cbf[:, 512:576]

        class Scr:
            def __init__(self):
                self.off = 0

            def reset(self):
                self.off = 0

            def alloc(self, nbytes, dt):
                ncell = (nbytes + 255) // 256
                c0 = self.off
                self.off += ncell
                assert self.off <= NCELL, self.off
                ap = scr[:, c0 * 128: c0 * 128 + nbytes // 2]
                if dt == F32:
                    ap = ap.bitcast(F32)
                return ap, [("scr", c) for c in range(c0, c0 + ncell)]

        SA = Scr()

        def csub(cells, lo, hi):
            return cells[lo // 256:(hi + 255) // 256]

        def mm(out, lhsT, rhs, start, stop, reads, writes, skip=False):
            P.op("pe", lambda e: e.matmul(out, lhsT=lhsT, rhs=rhs, start=start, stop=stop, skip_group_check=skip),
                 reads, writes)

        def tr(out, in_, idn, reads, writes):
            P.op("pe", lambda e: e.transpose(out=out, in_=in_, identity=idn), reads, writes)

        def act(out, in_, func, reads, writes, scale=1.0, bias=None, accum_out=None):
            kw = {}
            if bias is not None:
                kw["bias"] = bias
            if accum_out is not None:
                kw["accum_out"] = accum_out
            P.op("act", lambda e: e.activation(out=out, in_=in_, func=func, scale=scale, **kw), reads, writes)

        def vtt(out, in0, in1, op, reads, writes):
            P.op("dve", lambda e: e.tensor_tensor(out=out, in0=in0, in1=in1, op=op), reads, writes)

        def vstt(out, in0, scalar, in1, op0, op1, reads, writes):
            P.op("dve", lambda e: e.scalar_tensor_tensor(out=out, in0=in0, scalar=scalar, in1=in1, op0=op0, op1=op1),
                 reads, writes)

        def vcopy(out, in_, reads, writes, eng="dve"):
            P.op(eng, lambda e: e.tensor_copy(out=out, in_=in_), reads, writes)

        def memset(ap, val, writes, eng="pool"):
            P.op(eng, lambda e: e.memset(ap, val), (), writes)

        w_in, w_a, w_b, w_o, w_up, w_down = D["w_in"], D["w_a"], D["w_b"], D["w_o"], D["w_up"], D["w_down"]

        def kview(w, c0, n):
            return w[:, c0:c0 + n].rearrange("(k p) n -> p k n", p=128)

        def unit_dmas(kind, arg):
            if kind == "win":
                return [(lambda s: s[:, 0:4096].rearrange("p (k n) -> p k n", k=8), kview(w_in, arg, 512))]
            if kind == "wab":
                return [(lambda s: s[:, 0:2048].rearrange("p (k n) -> p k n", k=4), kview(w_a, arg, 512)),
                        (lambda s: s[:, 2048:4096].rearrange("p (k n) -> p k n", k=4), kview(w_b, arg, 512))]
            if kind == "wo":
                return [(lambda s: s[:, 0:4096].rearrange("p (k n) -> p k n", k=8), kview(w_o, arg, 512))]
            if kind == "up":
                return [(lambda s: s[:, 0:4096].rearrange("p (k t n) -> p k t n", k=8, t=2)[:, :, 0, :],
                         kview(w_up, 256 * arg, 256)),
                        (lambda s: s[:, 0:4096].rearrange("p (k t n) -> p k t n", k=8, t=2)[:, :, 1, :],
                         kview(w_up, 2816 + 256 * arg, 256))]
            if kind == "dn":
                return [(lambda s: s[:, 0:2816].rearrange("p (k n) -> p k n", k=22), kview(w_down, 128 * arg, 128))]
            raise ValueError(kind)

        tile_units = ([("win", 0), ("win", 512), ("win", 1024), ("win", 1536),
                       ("win", 2048), ("win", 3072), ("wab", 0),
                       ("win", 2560), ("win", 3584), ("wab", 512),
                       ("wo", 0), ("wo", 512)]
                      + [("up", i) for i in range(11)] + [("dn", c) for c in range(8)])
        if tiles is None:
            tiles = [("p", 0, 0), ("p", 0, 1), ("p", 0, 2), ("p", 0, 3),
                     ("p", 1, 0), ("p", 1, 1), ("p", 1, 2), ("p", 1, 3), ("s",)]
        all_units = tile_units * len(tiles)
        rstate = {"issued": 0, "cur": 0}

        NU = len(tile_units)
        wscr = nc.dram_tensor("wscr", [NU, 128, 4096], BF16, kind="Internal").ap()

        def unit_used(kind):
            return 2816 if kind == "dn" else 4096

        def ring_writeback(n):
            slot = n % NSLOT
            used = unit_used(all_units[n][0])
            P.dma("pool", wscr[n, :, 0:used], ring[:, slot, 0:used], reads=[("ring", slot)], writes=[("wscr", n)],
                  key=f"wb{slot}")

        def ring_issue():
            n = rstate["issued"]
            if n > 0 and n - 1 < NU and len(all_units) > NU:
                ring_writeback(n - 1)
            if n >= len(all_units):
                return
            slot = n % NSLOT
            sap = ring[:, slot, :]
            if n < NU:
                extra = [("dgethrottle4", n % 3)] + ([("dgethrottle", n % 2)] if all_units[n][0] == "dn" else [])
                for dstb, src in unit_dmas(*all_units[n]):
                    P.dma("pool", dstb(sap), src, writes=[("ring", slot)] + extra, key=f"ring{slot}")
            else:
                u = n % NU
                used = unit_used(all_units[n][0])
                P.dma("pool", ring[:, slot, 0:used], wscr[u, :, 0:used], reads=[("wscr", u)], writes=[("ring", slot)],
                      key=f"ring{slot}")
            rstate["issued"] = n + 1

        def ring_get(kind):
            n = rstate["cur"]
            assert all_units[n][0] == kind, (all_units[n], kind)
            rstate["cur"] = n + 1
            slot = n % NSLOT
            return ring[:, slot, :], ("ring", slot)

        def ring_done(k=1):
            for _ in range(k):
                ring_issue()

        memset(cbf[:], 0.0, ["cbf"])
        memset(identf[:], 0.0, ["identf"])
        memset(small[:, 15:16], 1e-6, ["smalleps"])
        memset(KT[:], 0.0, ["KT"])
        memset(Vb[:], 0.0, ["Vb"])
        memset(QTa[:], 0.0, ["QTa"])
        memset(QTb[:], 0.0, ["QTb"])

        def asel(out, pattern, cmp, fill, cm, res):
            P.op("pool", lambda e: e.affine_select(out=out, in_=out, pattern=pattern, compare_op=cmp, fill=fill,
                                                   base=0, channel_multiplier=cm), [res], [res])

        asel(cbf[:, 0:128], [[-1, 128]], ALU.not_equal, 1.0, 1, "cbf")
        asel(identf[:], [[-1, 128]], ALU.not_equal, 1.0, 1, "identf")
        asel(cbf[:, 128:256], [[1, 128]], ALU.is_ge, -1.0, -1, "cbf")
        asel(cbf[:, 256:384], [[-1, 128]], ALU.is_gt, -1.0, 1, "cbf")
        asel(cbf[:, 384:512], [[1, 128]], ALU.is_gt, -30000.0, -1, "cbf")
        asel(cbf[:, 512:576], [[0, 4], [1, 16]], ALU.is_gt, -30000.0, -1, "cbf")
        P.op("pool", lambda e: e.iota(invc[:], pattern=[[0, 4], [1, 15]], base=1, channel_multiplier=0,
                                      allow_small_or_imprecise_dtypes=True), (), ["invc"])
        for g in range(4):
            P.op("dve", lambda e, g=g: e.tensor_scalar_min(out=invc[:, g, :], in0=invc[:, g, :], scalar1=float(2 << g)),
                 ["invc"], ["invc"])
        P.op("dve", lambda e: e.reciprocal(out=invc[:], in_=invc[:]), ["invc"], ["invc"])
        P.dma("sp", gmix[:], D["norm_mix"].partition_broadcast(128), writes=["gmix"], key="c0")
        P.dma("sp", gffn[:], D["norm_ffn"].partition_broadcast(128), writes=["gffn"], key="c1")
        P.dma("sp", gfin[:], D["norm_final"].partition_broadcast(128), writes=["gfin"], key="c2")
        for j in range(3):
            P.dma("sp", cparst[:, j, :], D["conv_w"][j].rearrange("(k c) -> k c", c=128), writes=["cparst"], key="c3")
        P.dma("sp", cparst[:, 3, :], D["conv_b"].rearrange("(k c) -> k c", c=128), writes=["cparst"], key="c3")
        P.dma("sp", pscst[:], D["pool_scale"].rearrange("(g c) -> g c", c=128), writes=["pscst"], key="c4")
        for j in range(4):
            tr(banks[7][:, j * 44:(j + 1) * 44], cparst[:, j, :], identf[0:44, 0:44], ["cparst", "identf"], [bankr[7]])
        tr(banks[7][:, 176:180], pscst[:], identf[0:4, 0:4], ["pscst", "identf"], [bankr[7]])
        vcopy(cpar[:], banks[7][:, 0:180], [bankr[7]], ["cpar"])
        P.dma("pool", poolw[:], D["pool_w"].rearrange("g c d -> c g d"), writes=["poolw"], key="c5")
        for _ in range(NSLOT):
            ring_issue()

        eps_ap = small[:, 15:16]
        nctr = {"n": 0}

        def rmsnorm(xin, rows, gb, gres, out, xres, outres):
            k = nctr["n"] % 4
            nctr["n"] += 1
            ms_ap, lt_ap, rstd_ap = small[:, 3 * k:3 * k + 1], small[:, 3 * k + 1:3 * k + 2], small[:, 3 * k + 2:3 * k + 3]
            sr = [("small", k)]
            act(hb[0:rows, :], xin, AF.Square, xres, ["hb"] + sr, scale=1.0 / 32.0, accum_out=ms_ap[0:rows])
            act(lt_ap[0:rows], ms_ap[0:rows], AF.Ln, sr + ["smalleps"], sr, bias=eps_ap[0:rows])
            act(rstd_ap[0:rows], lt_ap[0:rows], AF.Exp, sr, sr, scale=-0.5)
            vstt(out, xin, rstd_ap[0:rows], gb[0:rows, :], ALU.mult, ALU.mult, list(xres) + sr + [gres], outres)

        dctr = {"n": 0}
        DBANKS = [7, 0, 1, 2, 3, 4, 5, 6]

        def dbank():
            i = DBANKS[dctr["n"] % len(DBANKS)]
            dctr["n"] += 1
            return banks[i], bankr[i]

        vhoist = {"done": False}
        pending_y = []

        def flush_y():
            while pending_y:
                P.push(pending_y.pop(0))

        def tile_geom(tile):
            if tile[0] == "p":
                return [(tb, 128) for tb in range(4)]
            return [(0, 64)]

        def load_x(ti):
            tile = tiles[ti]
            xb = xt2[:, ti % 2]
            if tile[0] == "p":
                _, b_, j_ = tile
                P.dma("sp", xb, D["xp"][b_, 512 * j_:512 * j_ + 512, :].rearrange("(t p) d -> p t d", p=128),
                      writes=[("x", ti % 2, tb) for tb in range(4)], key=f"x{ti % 2}")
            else:
                P.dma("sp", xb[0:64, 0, :], D["xs"], writes=[("x", ti % 2, 0)], key=f"x{ti % 2}")

        def norm_block_a(ti, tb, rows, gb, gres):
            xb = xt2[:, ti % 2]
            rmsnorm(xb[0:rows, tb, :], rows, gb, gres, hb[0:rows, :], [("x", ti % 2, tb)], ["hb"])

        def norm_block_b(ti, tb, rows):
            bk, br = dbank()
            pT = bk[:].bitcast(BF16)
            for kc in range(8):
                tr(pT[:, kc * 128: kc * 128 + rows], hb[0:rows, kc * 128:(kc + 1) * 128], ident[0:rows, 0:rows],
                   ["hb", "cbf"], [br])
            src = pT.rearrange("p (k n) -> p k n", k=8)[:, :, 0:rows]
            act(hT[:, :, tb * 128: tb * 128 + rows], src, AF.Copy, [br], [("hT", tb)])

        def norm_block(ti, tb, rows, gb, gres):
            norm_block_a(ti, tb, rows, gb, gres)
            norm_block_b(ti, tb, rows)

        def run_tile(ti):
            tile = tiles[ti]
            xt = xt2[:, ti % 2]
            prompt = tile[0] == "p"
            if prompt:
                _, b, j = tile
                nseq, tl, NT = 1, 512, 512
                blocks = [(tb, 128) for tb in range(4)]
                tok0 = 512 * j
            else:
                nseq, tl, NT = 4, 16, 64
                blocks = [(0, 64)]
                b, j = None, None
            E = 15 + tl
            xres = lambda tb: [("x", ti % 2, tb)]
            hTres = [("hT", tb) for tb, _ in blocks]

            def norm_to_hT(gb, gres):
                for tb, rows in blocks:
                    norm_block(ti, tb, rows, gb, gres)


            SA.reset()
            ebuf = [SA.alloc(2048, F32) for _ in range(2)]
            lnwb = [SA.alloc(1024, BF16) for _ in range(6)]
            lsigb = [SA.alloc(2048, F32) for _ in range(4)]
            tmpb = [SA.alloc(2048, F32) for _ in range(2)]
            Ab = [SA.alloc(1024, BF16) for _ in range(3)]
            if prompt:
                pmixT_ap, pmixr = SA.alloc(4 * NT * 2, BF16)
                pmixT = pmixT_ap.rearrange("p (g n) -> p g n", g=4)
            stage_off = SA.off
            kst = [SA.alloc(2048, F32) for _ in range(2)]
            vst = [SA.alloc(2048, F32) for _ in range(2)]
            b16 = [SA.alloc(1024, BF16) for _ in range(4)]
            cachek = SA.alloc(8192, BF16) if not prompt else None
            ktn = SA.alloc(512, BF16) if not prompt else None
            vnew = SA.alloc(4096, BF16) if not prompt else None
            rot = {"k": 0, "v": 0, "b": 0}

            def proj_tm(slot, sres, tb, rows, col_lo=0):
                bk, br = dbank()
                sv = slot[:, 0:4096].rearrange("p (k n) -> p k n", k=8)
                for kc in range(8):
                    mm(bk[0:rows, :], hT[:, kc, tb * 128 + col_lo: tb * 128 + col_lo + rows], sv[:, kc, :],
                       kc == 0, kc == 7, [("hT", tb), sres], [br])
                return bk, br

            slot, sres = ring_get("win")

            def q_post(st):
                tb, rows, qb, qr = st
                bk2, br2 = dbank()
                pT = bk2[:].bitcast(BF16)
                for m in range(4):
                    tr(pT[:, m * 128: m * 128 + rows], qb[0:rows, m * 128:(m + 1) * 128], ident[0:rows, 0:rows],
                       qr + ["cbf"], [br2])
                src = pT[:, 0:512].rearrange("p (m n) -> p m n", m=4)
                vcopy(QTa[0:64, :, tb * 128: tb * 128 + rows], src[0:64, :, 0:rows], [br2], ["QTa"])
                vcopy(QTb[64:128, :, tb * 128: tb * 128 + rows], src[64:128, :, 0:rows], [br2], ["QTb"])

            prev = None
            for tb, rows in blocks:
                bk, br = proj_tm(slot, sres, tb, rows)
                qb, qr = b16[rot["b"] % 2]; rot["b"] += 1
                act(qb[0:rows, :], bk[0:rows, :], AF.Copy, [br], qr, scale=0.125)
                if prev is not None:
                    q_post(prev)
                prev = (tb, rows, qb, qr)
            ring_done()
            qprev = prev
            slot, sres = ring_get("win")

            def k_post(st):
                tb, rows, kb_, kbr = st
                bk2, br2 = dbank()
                pT = bk2[:].bitcast(BF16)
                for m in range(4):
                    tr(pT[:, m * 128: m * 128 + rows], kb_[0:rows, m * 128:(m + 1) * 128], ident[0:rows, 0:rows],
                       kbr + ["cbf"], [br2])
                src = pT[:, 0:512].rearrange("p (m n) -> p m n", m=4)
                if prompt:
                    c0 = tok0 + tb * 128
                    act(KT[:, :, c0:c0 + 128], src, AF.Copy, [br2], [("KT", (tok0 // 128) + tb)])
                else:
                    act(ktn[0].rearrange("p (m n) -> p m n", m=4), src[:, :, 0:64], AF.Copy, [br2], ktn[1])

            prev = None
            for tb, rows in blocks:
                bk, br = proj_tm(slot, sres, tb, rows)
                if qprev is not None:
                    q_post(qprev)
                    qprev = None
                ks_, ksr = kst[rot["k"] % 2]; rot["k"] += 1
                act(ks_[0:rows, :], bk[0:rows, :], AF.Copy, [br], ksr)
                kb_, kbr = b16[2 + rot["b"] % 2]; rot["b"] += 1
                vcopy(kb_[0:rows, :], ks_[0:rows, :], ksr, kbr)
                if prompt:
                    P.dma("sp", D["kp"][b, tok0 + tb * 128: tok0 + tb * 128 + 128, :], ks_[:, :], reads=ksr,
                          writes=["kp_out"], key=f"kst{(rot['k'] - 1) % 2}")
                else:
                    P.dma("sp", D["ks"], ks_[0:64, :], reads=ksr, writes=["ks_out"], key=f"kst{(rot['k'] - 1) % 2}")
                if prev is not None:
                    k_post(prev)
                prev = (tb, rows, kb_, kbr)
            ring_done()
            kprev = prev
            slot, sres = ring_get("win")
            if prompt:
                for tb, rows in blocks:
                    bk, br = proj_tm(slot, sres, tb, rows)
                    if kprev is not None:
                        k_post(kprev)
                        kprev = None
                    vs_, vsr = vst[rot["v"] % 2]; rot["v"] += 1
                    act(vs_[0:rows, :], bk[0:rows, :], AF.Copy, [br], vsr)
                    blk = (tok0 // 128) + tb
                    vcopy(Vb[:, blk, :], vs_[:, :], vsr, [("V", blk)])
                    P.dma("sp", D["vp"][b, tok0 + tb * 128: tok0 + tb * 128 + 128, :], vs_[:, :], reads=vsr,
                          writes=["vp_out"], key=f"vst{(rot['v'] - 1) % 2}")
            else:
                for s in range(4):
                    bk, br = proj_tm(slot, sres, 0, 16, col_lo=s * 16)
                    if kprev is not None:
                        k_post(kprev)
                        kprev = None
                    vs_, vsr = vst[rot["v"] % 2]; rot["v"] += 1
                    act(vs_[0:16, :], bk[0:16, :], AF.Copy, [br], vsr)
                    vcopy(vnew[0].rearrange("p (s n) -> p s n", s=4)[0:16, s, :], vs_[0:16, :], vsr, vnew[1])
                    P.dma("sp", D["vs"][s * 16:(s + 1) * 16, :], vs_[0:16, :], reads=vsr, writes=["vs_out"],
                          key=f"vst{(rot['v'] - 1) % 2}")
            ring_done()
            flush_y()
            if ti + 1 < len(tiles):
                load_x(ti + 1)

            P4 = {}

            def p4_alloc(AL):
                uT_ap, uTr = AL.alloc(4 * nseq * E * 4, F32)
                sA_ap, sAr = AL.alloc(nseq * E * 4, F32)
                sB_ap, sBr = AL.alloc(nseq * E * 4, F32)
                pooledT_ap, pooledr = AL.alloc(4 * NT * 2, BF16)
                t15_ap, t15r = AL.alloc(64, F32)
                pst_ap, pstr = AL.alloc(2048, F32)
                P4.update(uT=uT_ap.rearrange("p (g s e) -> p g s e", g=4, s=nseq), uTr=uTr,
                          sA=sA_ap.rearrange("p (s e) -> p s e", s=nseq), sAr=sAr,
                          sB=sB_ap.rearrange("p (s e) -> p s e", s=nseq), sBr=sBr,
                          pooledT=pooledT_ap.rearrange("p (g n) -> p g n", g=4), pooledr=pooledr,
                          t15_ap=t15_ap, t15r=t15r, pst_ap=pst_ap, pstr=pstr)

            def u_pe():
                uT, uTr, pst_ap, pstr = P4["uT"], P4["uTr"], P4["pst_ap"], P4["pstr"]
                if prompt:
                    if j == 0:
                        memset(uT[:, :, 0, 0:15], 0.0, uTr, eng="dve")
                    else:
                        vcopy(uT[:, :, 0, 0:15], uctx[:], ["uctx"], uTr)
                else:
                    for s in range(4):
                        P.dma("sp", pst_ap[0:15, :], D["spool"][s], writes=pstr, key="pst")
                        bk, br = dbank()
                        for g in range(4):
                            tr(bk[:, g * 16:(g + 1) * 16], pst_ap[0:16, g * 128:(g + 1) * 128], identf[0:16, 0:16],
                               pstr + ["identf"], [br])
                        vcopy(uT[:, :, s, 0:15], bk[:, 0:64].rearrange("p (g r) -> p g r", g=4)[:, :, 0:15], [br], uTr)
                slot, sres = ring_get("win")
                sv = slot[:, 0:4096].rearrange("p (k n) -> p k n", k=8)
                for g in range(4):
                    bk, br = dbank()
                    for kc in range(8):
                        mm(bk[:, 0:NT], sv[:, kc, g * 128:(g + 1) * 128], hT[:, kc, 0:NT], kc == 0, kc == 7,
                           hTres + [sres], [br])
                    act(uT[:, g, :, 15:E], bk[:, 0:NT].rearrange("p (s t) -> p s t", s=nseq), AF.Copy, [br], uTr)
                ring_done()
                if prompt and j < 3:
                    vcopy(uctx[:], uT[:, :, 0, tl:E], uTr, ["uctx"])
                if (prompt and j == 3) or not prompt:
                    for s in range(nseq):
                        bk, br = dbank()
                        for g in range(4):
                            tr(bk[0:15, g * 128:(g + 1) * 128], uT[:, g, s, tl:E], identf[:], uTr + ["identf"], [br])
                        vcopy(pst_ap[0:15, :], bk[0:15, :], [br], pstr)
                        dst = D["pp"][b] if prompt else D["ps"][s]
                        P.dma("sp", dst, pst_ap[0:15, :], reads=pstr, writes=["pool_out"], key="pst")

            def pool_group(g, fixed_bank=None):
                uT, uTr, sA, sAr, sB, sBr = P4["uT"], P4["uTr"], P4["sA"], P4["sAr"], P4["sB"], P4["sBr"]
                pooledT, pooledr, t15_ap, t15r = P4["pooledT"], P4["pooledr"], P4["t15_ap"], P4["t15r"]
                w = 2 << g
                ug = uT[:, g, :, :]
                vtt(sA[:, :, 1:E], ug[:, :, 1:E], ug[:, :, 0:E - 1], ALU.add, uTr, sAr)
                fin = sA
                finr = sAr
                if g >= 1:
                    vtt(sB[:, :, 3:E], sA[:, :, 3:E], sA[:, :, 1:E - 2], ALU.add, sAr, sBr)
                    fin, finr = sB, sBr
                if g >= 2:
                    vtt(sA[:, :, 7:E], sB[:, :, 7:E], sB[:, :, 3:E - 4], ALU.add, sBr, sAr)
                    fin, finr = sA, sAr
                if g >= 3:
                    vtt(sB[:, :, 15:E], sA[:, :, 15:E], sA[:, :, 7:E - 8], ALU.add, sAr, sBr)
                    fin, finr = sB, sBr
                pg = pooledT[:, g, 0:NT].rearrange("p (s t) -> p s t", s=nseq)
                vstt(pg, fin[:, :, 15:E], 1.0 / w, ug[:, :, 15:E], ALU.mult, ALU.subtract, finr + uTr, pooledr)
                if prompt and j == 0:
                    vtt(t15_ap[:, 0:15], fin[:, 0, 15:30], invc[:, g, :], ALU.mult, finr + ["invc"], t15r)
                    vtt(pooledT[:, g, 0:15], t15_ap[:, 0:15], ug[:, 0, 15:30], ALU.subtract, t15r + uTr, pooledr)
                bk, br = dbank() if fixed_bank is None else (banks[fixed_bank], bankr[fixed_bank])
                mm(bk[:, 0:NT], poolw[:, g, :], pooledT[:, g, 0:NT], True, True, ["poolw"] + pooledr, [br])
                act(pmixT[:, g, 0:NT], bk[:, 0:NT], AF.Identity, [br, "cpar"], pmixr, scale=cpar[:, 176 + g:177 + g])

            ictr = {"n": 0}

            def attention(items, extras=None):
                n = len(items)
                base = ictr["n"]
                ictr["n"] += n

                def bufs(i):
                    g = base + i
                    return (banks[g % 3], bankr[g % 3], ebuf[g % 2], lnwb[g % 6], lsigb[g % 4], tmpb[g % 2], Ab[g % 3])

                def pe_qk(i, it):
                    Z, Zr = bufs(i)[0:2]
                    hp = it["hp"]
                    QT, QTr = (QTa, "QTa") if hp == 0 else (QTb, "QTb")
                    nq = len(it["qk"])
                    for idx, (m, zc0, nn, qc0) in enumerate(it["qk"]):
                        mm(Z[:, zc0:zc0 + nn], KT[:, m, it["kcol"]:it["kcol"] + 128], QT[:, m, qc0:qc0 + nn],
                           idx == 0, (idx == nq - 1) and not it["diag"], [QTr] + it["kres"], [Zr], skip=True)
                    if it["diag"]:
                        mk, mc0, mn = it["mask"]
                        mm(Z[:, mc0:mc0 + mn], ident, mk, False, True, ["cbf"], [Zr], skip=True)

                def act_exp(i, it):
                    Z, Zr, (e_, er) = bufs(i)[0:3]
                    c0, c1 = it["c0"], it["c1"]
                    act(e_[:, c0:c1], Z[:, c0:c1], AF.Exp, [Zr], er)

                def act_ln(i, it):
                    _, _, (e_, er), (ln_, lnr) = bufs(i)[0:4]
                    c0, c1 = it["c0"], it["c1"]
                    act(ln_[:, c0:c1], e_[:, c0:c1], AF.Ln, er, lnr, bias=1.0)

                def dve_lsig(i, it):
                    Z, Zr, _, (ln_, lnr), (ls_, lsr) = bufs(i)[0:5]
                    c0, c1 = it["c0"], it["c1"]
                    vtt(ls_[:, c0:c1], Z[:, c0:c1], ln_[:, c0:c1], ALU.subtract, [Zr] + lnr, lsr)

                def pe_mm1(i, it):
                    (ln_, lnr) = bufs(i)[3]
                    hp = it["hp"]
                    c0, c1 = it["c0"], it["c1"]
                    LB, LBr = banks[3 + hp], bankr[3 + hp]
                    mm(LB[:, c0:c1], negU, ln_[:, c0:c1], it["first"], False, lnr + ["cbf"], [LBr], skip=True)

                def dve_tmp(i, it):
                    _, _, _, _, (ls_, lsr), (t_, tr_), _ = bufs(i)
                    hp = it["hp"]
                    c0, c1 = it["c0"], it["c1"]
                    LB, LBr = banks[3 + hp], bankr[3 + hp]
                    vtt(t_[:, c0:c1], LB[:, c0:c1], ls_[:, c0:c1], ALU.add, [LBr] + lsr, tr_)

                def act_A(i, it):
                    _, _, _, _, _, (t_, tr_), (A_, Ar) = bufs(i)
                    c0, c1 = it["c0"], it["c1"]
                    act(A_[:, c0:c1], t_[:, c0:c1], AF.Exp, tr_, Ar)

                def pe_mm2(i, it):
                    (ln_, lnr) = bufs(i)[3]
                    hp = it["hp"]
                    c0, c1 = it["c0"], it["c1"]
                    LB, LBr = banks[3 + hp], bankr[3 + hp]
                    if not it["last"]:
                        mm(LB[:, c0:c1], negL, ln_[:, c0:c1], False, False, lnr + ["cbf"], [LBr], skip=True)

                def pe_av(i, it):
                    (A_, Ar) = bufs(i)[6]
                    hp = it["hp"]
                    O, Or = banks[5 + hp], bankr[5 + hp]
                    for idx, (m, zc0, nn, qc0) in enumerate(it["qk"]):
                        mm(O[:, zc0:zc0 + nn], Vb[:, it["vblk"], m * 128:(m + 1) * 128], A_[:, zc0:zc0 + nn],
                           it["first"] and idx == 0, it["last"], Ar + it["vres"], [Or], skip=True)
                    if it["last"]:
                        it["evac"](O, Or, hp)

                def at(fn, k):
                    if 0 <= k < n:
                        fn(k, items[k])

                for t in range(n + 6):
                    at(pe_mm2, t - 4)
                    at(pe_av, t - 5)
                    at(pe_mm1, t - 2)
                    at(pe_qk, t)
                    at(act_exp, t - 1)
                    at(act_A, t - 4)
                    at(act_ln, t - 1)
                    at(dve_lsig, t - 2)
                    at(dve_tmp, t - 3)
                    if extras and t >= 3:
                        P.push(extras.pop(0))
                if extras:
                    for o_ in extras:
                        P.push(o_)

            if prompt:
                items = []
                for m in range(4):
                    def evac(O, Or, hp, m=m):
                        act(attnT[hp * 64:(hp + 1) * 64, m, 0:512], O[hp * 64:(hp + 1) * 64, 0:512], AF.Copy,
                            [Or], [("attnT", m, hp)])
                    kmax = 4 * j + 3
                    for kb in range(kmax, -1, -1):
                        for hp in range(2):
                            diag = kb >= 4 * j
                            c0 = 128 * (kb - 4 * j) if diag else 0
                            items.append(dict(hp=hp, c0=c0, c1=512, qk=[(m, c0, 512 - c0, c0)], diag=diag,
                                              mask=(maskD, c0, 128), kcol=kb * 128, vblk=kb,
                                              kres=[("KT", kb)], vres=[("V", kb)],
                                              first=(kb == kmax), last=(kb == 0), evac=evac))
                SA2 = Scr()
                SA2.off = stage_off
                p4_alloc(SA2)
                u_pe()
                P.begin_defer()
                for g in range(4):
                    pool_group(g, fixed_bank=7)
                extras = P.end_defer()
                attention(items, extras)
                if ti + 1 < len(tiles) and tiles[ti + 1][0] == "s":
                    for s in range(2):
                        P.dma("pool", Vb[:, s * 9: s * 9 + 8, :], D["cv"][s].rearrange("(b p) n -> p b n", p=128),
                              writes=[("V", s * 9 + q) for q in range(8)], key=f"vc{s}")
                    vhoist["done"] = True
            else:
                ck_ap, ckr = cachek
                ck3 = ck_ap.rearrange("p (b n) -> p b n", b=8)
                for s in range(4):
                    half = s % 2
                    if not (vhoist["done"] and s < 2):
                        P.dma("pool", Vb[:, half * 9: half * 9 + 8, :], D["cv"][s].rearrange("(b p) n -> p b n", p=128),
                              writes=[("V", half * 9 + q) for q in range(8)], key=f"vc{half}")
                    P.dma("pool", ck3, D["ck"][s].rearrange("(b p) n -> p b n", p=128), writes=ckr, key="ckc")
                    for blk in range(8):
                        bk2, br2 = dbank()
                        pT = bk2[:].bitcast(BF16)
                        for m in range(4):
                            tr(pT[:, m * 128:(m + 1) * 128], ck3[:, blk, m * 128:(m + 1) * 128], ident,
                               ckr + ["cbf"], [br2])
                        c0 = half * 1152 + blk * 128
                        act(KT[:, :, c0:c0 + 128], pT[:, 0:512].rearrange("p (m n) -> p m n", m=4), AF.Copy,
                            [br2], [("KT", half * 9 + blk)])
                    c0n = half * 1152 + 1024
                    act(KT[:, :, c0n:c0n + 16], ktn[0].rearrange("p (m n) -> p m n", m=4)[:, :, s * 16:(s + 1) * 16],
                        AF.Copy, ktn[1], [("KT", half * 9 + 8)])
                    vcopy(Vb[0:16, half * 9 + 8, :], vnew[0].rearrange("p (s n) -> p s n", s=4)[0:16, s, :], vnew[1],
                          [("V", half * 9 + 8)])
                    items = []

                    def evac(O, Or, hp, s=s):
                        act(attnT[hp * 64:(hp + 1) * 64, :, s * 16:(s + 1) * 16],
                            O[hp * 64:(hp + 1) * 64, 0:64].rearrange("p (m n) -> p m n", m=4), AF.Copy,
                            [Or], [("attnT", s, hp)])
                    for kb in range(8, -1, -1):
                        for hp in range(2):
                            items.append(dict(hp=hp, c0=0, c1=64, qk=[(m, 16 * m, 16, s * 16) for m in range(4)],
                                              diag=(kb == 8), mask=(maskS4, 0, 64), kcol=half * 1152 + kb * 128,
                                              vblk=half * 9 + kb, kres=[("KT", half * 9 + kb)],
                                              vres=[("V", half * 9 + kb)],
                                              first=(kb == 8), last=(kb == 0), evac=evac))
                    attention(items)
            attn_res = ([("attnT", m, hp) for m in range(4) for hp in range(2)] if prompt
                        else [("attnT", s, hp) for s in range(4) for hp in range(2)])

            if not prompt:
                SA.reset()
                p4_alloc(SA)
                pmixT_ap, pmixr = SA.alloc(4 * NT * 2, BF16)
                pmixT = pmixT_ap.rearrange("p (g n) -> p g n", g=4)
                u_pe()
                for g in range(4):
                    pool_group(g)
            else:
                SA.reset()
            mergedT_ap, mergedr = SA.alloc(8 * NT * 2, BF16)
            mergedT = mergedT_ap.rearrange("p (k n) -> p k n", k=8)
            sg = [SA.alloc(NT * 4, F32) for _ in range(4)]
            m1b = [SA.alloc(NT * 4, F32) for _ in range(4)]

            for half in range(2):
                gaS, gar = ring_get("win")
                gbS, gbr = ring_get("win")
                abS, abr = ring_get("wab")
                gav = gaS[:, 0:4096].rearrange("p (k n) -> p k n", k=8)
                gbv = gbS[:, 0:4096].rearrange("p (k n) -> p k n", k=8)
                wav = abS[:, 0:2048].rearrange("p (k n) -> p k n", k=4)
                wbv = abS[:, 2048:4096].rearrange("p (k n) -> p k n", k=4)
                for cc in range(4):
                    c = half * 4 + cc
                    cs = slice(cc * 128, (cc + 1) * 128)
                    bk, br = dbank()
                    for kc in range(8):
                        mm(bk[:, 0:NT], gav[:, kc, cs], hT[:, kc, 0:NT], kc == 0, kc == 7, hTres + [gar], [br])
                    (sga, sgar) = sg[2 * (c % 2)]
                    act(sga[:, 0:NT], bk[:, 0:NT], AF.Sigmoid, [br], sgar)
                    bk, br = dbank()
                    for kc in range(4):
                        mm(bk[:, 0:NT], wav[:, kc, cs], attnT[:, kc, 0:NT], kc == 0, kc == 3, attn_res + [abr], [br])
                    (m1, m1r) = m1b[2 * (c % 2)]
                    vtt(m1[:, 0:NT], bk[:, 0:NT], sga[:, 0:NT], ALU.mult, [br] + sgar, m1r)
                    bk, br = dbank()
                    for kc in range(8):
                        mm(bk[:, 0:NT], gbv[:, kc, cs], hT[:, kc, 0:NT], kc == 0, kc == 7, hTres + [gbr], [br])
                    (sgb, sgbr) = sg[2 * (c % 2) + 1]
                    act(sgb[:, 0:NT], bk[:, 0:NT], AF.Sigmoid, [br], sgbr)
                    bk, br = dbank()
                    for kc in range(4):
                        mm(bk[:, 0:NT], wbv[:, kc, cs], pmixT[:, kc, 0:NT], kc == 0, kc == 3, pmixr + [abr], [br])
                    (m2, m2r) = m1b[2 * (c % 2) + 1]
                    vtt(m2[:, 0:NT], bk[:, 0:NT], sgb[:, 0:NT], ALU.mult, [br] + sgbr, m2r)
                    vtt(mergedT[:, c, 0:NT], m1[:, 0:NT], m2[:, 0:NT], ALU.add, m1r + m2r, [("merged", c)])
                ring_done(3)
            mres = [("merged", c) for c in range(8)]
            wos = [ring_get("wo"), ring_get("wo")]
            prevb = None
            for tb, rows in blocks:
                for hh in range(2):
                    slot, sres = wos[hh]
                    sv = slot[:, 0:4096].rearrange("p (k n) -> p k n", k=8)
                    bk, br = dbank()
                    for kc in range(8):
                        mm(bk[0:rows, :], mergedT[:, kc, tb * 128: tb * 128 + rows], sv[:, kc, :], kc == 0, kc == 7,
                           mres + mergedr + [sres], [br])
                    xs_ = xt[0:rows, tb, hh * 512:(hh + 1) * 512]
                    vtt(xs_, bk[0:rows, :], xs_, ALU.add, [br, ("x", ti % 2, tb)], [("x", ti % 2, tb)])
                if prevb is not None:
                    norm_block_b(ti, prevb[0], prevb[1])
                norm_block_a(ti, tb, rows, gffn, "gffn")
                prevb = (tb, rows)
            ring_done(2)
            norm_block_b(ti, prevb[0], prevb[1])

            SA.reset()
            actT_ap, actr = SA.alloc(22 * NT * 2, BF16)
            actT = actT_ap.rearrange("p (k n) -> p k n", k=22)
            accb = [SA.alloc(NT * 4, F32) for _ in range(6)]
            glb = [SA.alloc(NT * 4, F32) for _ in range(2)]
            y2b = [SA.alloc(NT * 4, F32) for _ in range(2)]
            cst_ap, cstr = SA.alloc(1024, F32)
            cst3 = cst_ap.rearrange("p (r c) -> p r c", r=2)
            if prompt and j == 0:
                memset(ctx[:, 0, :, :], 0.0, [("ctx", ch) for ch in range(44)], eng="dve")
            if not prompt:
                for s in range(4):
                    P.dma("sp", cst3[0:44, :, :], D["sconv"][s].rearrange("r (k c) -> k r c", c=128), writes=cstr,
                          key="cst")
                    bk, br = dbank()
                    for r in range(2):
                        tr(bk[:, r * 44:(r + 1) * 44], cst3[0:44, r, :], identf[0:44, 0:44], cstr + ["identf"], [br])
                    vcopy(ctx[:, s, :, :], bk[:, 0:88].rearrange("p (r k) -> p r k", r=2), [br], [("ctx", ch) for ch in range(44)])
            ctxall = [("ctx", ch) for ch in range(44)]
            cctr = {"n": 0}
            bc_ap, bcr = SA.alloc(nseq * 2 * 44 * 4, F32)
            bct_ap, bctr = SA.alloc(44 * 4, F32)
            bc = bc_ap.rearrange("p (s r k) -> p s r k", s=nseq, r=2)
            w0v, w1v, bv = cpar[:, 0:44], cpar[:, 44:88], cpar[:, 132:176]
            for s_ in range(nseq):
                vtt(bc[:, s_, 0, :], ctx[:, s_, 1, :], w1v, ALU.mult, ctxall + ["cpar"], bcr)
                vtt(bct_ap[:, 0:44], ctx[:, s_, 0, :], w0v, ALU.mult, ctxall + ["cpar"], bctr)
                vtt(bc[:, s_, 0, :], bc[:, s_, 0, :], bct_ap[:, 0:44], ALU.add, bcr + bctr, bcr)
                vtt(bc[:, s_, 1, :], ctx[:, s_, 1, :], w0v, ALU.mult, ctxall + ["cpar"], bcr)

            def actres(kc):
                return csub(actr, kc * NT * 2, (kc + 1) * NT * 2)

            def conv_a(bk, br, ch):
                i = cctr["n"] % 6
                cctr["n"] += 1
                acc_ap, accr = accb[i]
                acc = acc_ap[:, 0:NT].rearrange("p (s t) -> p s t", s=nseq)
                src = bk[:, 0:NT].rearrange("p (s t) -> p s t", s=nseq)
                act(acc, src, AF.Identity, [br, "cpar"], accr, scale=cpar[:, 88 + ch:89 + ch],
                    bias=cpar[:, 132 + ch:133 + ch])
                act(ctx[:, 0:nseq, :, ch], src[:, :, tl - 2:tl], AF.Copy, [br], [("ctx", ch)])
                return (src, br, acc, accr, acc_ap, ch)

            def conv_d(st):
                src, br, acc, accr, acc_ap, ch = st
                P.op("pool", lambda e: e.tensor_tensor(out=acc[:, :, 0:2], in0=acc[:, :, 0:2], in1=bc[:, :, :, ch],
                                                       op=ALU.add), accr + bcr, accr)

            def conv_b(st):
                src, br, acc, accr, acc_ap, ch = st
                vstt(acc[:, :, 1:tl], src[:, :, 0:tl - 1], cpar[:, 44 + ch:45 + ch], acc[:, :, 1:tl], ALU.mult, ALU.add,
                     [br, "cpar"] + accr, accr)

            def conv_c(st):
                src, br, acc, accr, acc_ap, ch = st
                vstt(acc[:, :, 2:tl], src[:, :, 0:tl - 2], cpar[:, ch:ch + 1], acc[:, :, 2:tl], ALU.mult, ALU.add,
                     [br, "cpar"] + accr, accr)

            def gelu_mul(pend):
                cg, sts = pend
                (gl, glr) = glb[cg % 2]
                act(gl[:, 0:NT], sts[0][4][:, 0:NT], AF.Gelu_apprx_tanh, sts[0][3], glr)
                P.op("pool", lambda e: e.tensor_tensor(out=actT[:, cg, 0:NT], in0=gl[:, 0:NT], in1=sts[1][4][:, 0:NT],
                                                       op=ALU.mult), glr + sts[1][3], actres(cg))

            pending = None
            for i in range(11):
                slot, sres = ring_get("up")
                sv = slot[:, 0:4096].rearrange("p (k t n) -> p k t n", k=8, t=2)
                for a in range(2):
                    cg = 2 * i + a
                    sts = []
                    for t in range(2):
                        bk, br = dbank()
                        for kc in range(8):
                            mm(bk[:, 0:NT], sv[:, kc, t, a * 128:(a + 1) * 128], hT[:, kc, 0:NT], kc == 0, kc == 7,
                               hTres + [sres], [br])
                        sts.append(conv_a(bk, br, cg + 22 * t))
                    conv_b(sts[0]); conv_b(sts[1])
                    if pending is not None:
                        gelu_mul(pending)
                    conv_c(sts[0]); conv_c(sts[1])
                    conv_d(sts[0]); conv_d(sts[1])
                    pending = (cg, sts)
                ring_done()
            gelu_mul(pending)
            if (prompt and j == 3) or not prompt:
                for s in range(nseq):
                    bk, br = dbank()
                    for r in range(2):
                        tr(bk[0:44, r * 128:(r + 1) * 128], ctx[:, s, r, :], identf[:], [("ctx", ch) for ch in range(44)] + ["identf"], [br])
                    vcopy(cst3[0:44, :, :], bk[0:44, 0:256].rearrange("p (r c) -> p r c", r=2), [br], cstr)
                    dst = D["cp"][b] if prompt else D["cs"][s]
                    P.dma("sp", dst.rearrange("r (k c) -> k r c", c=128), cst3[0:44, :, :], reads=cstr,
                          writes=["conv_out"], key="cst")
            nblocks = tile_geom(tiles[ti + 1]) if ti + 1 < len(tiles) else []

            def dn_post(st):
                c, y2, y2r = st
                bk2, br2 = dbank()
                for tb, rows in blocks:
                    tr(bk2[0:rows, tb * 128:(tb + 1) * 128], y2[:, tb * 128: tb * 128 + rows], identf[:],
                       y2r + ["identf"], [br2])
                if prompt:
                    xv = xt[:, :, c * 128:(c + 1) * 128]
                    vtt(xv, bk2[:, :].rearrange("p (t n) -> p t n", t=4), xv, ALU.add,
                        [br2] + [("x", ti % 2, tb) for tb in range(4)], [("x", ti % 2, tb) for tb in range(4)])
                else:
                    xv = xt[0:64, 0, c * 128:(c + 1) * 128]
                    vtt(xv, bk2[0:64, 0:128], xv, ALU.add, [br2, ("x", ti % 2, 0)], [("x", ti % 2, 0)])

            prev = None
            for c in range(8):
                slot, sres = ring_get("dn")
                sv = slot[:, 0:2816].rearrange("p (k n) -> p k n", k=22)
                bk, br = dbank()
                for kc in range(22):
                    mm(bk[:, 0:NT], sv[:, kc, :], actT[:, kc, 0:NT], kc == 0, kc == 21, actres(kc) + [sres], [br])
                (y2, y2r) = y2b[c % 2]
                act(y2[:, 0:NT], bk[:, 0:NT], AF.Copy, [br], y2r)
                if prev is not None:
                    dn_post(prev)
                if 1 <= c <= len(nblocks):
                    norm_block_b(ti + 1, nblocks[c - 1][0], nblocks[c - 1][1])
                if c < len(nblocks):
                    norm_block_a(ti + 1, nblocks[c][0], nblocks[c][1], gmix, "gmix")
                prev = (c, y2, y2r)
                ring_done()
            dn_post(prev)

            for tb, rows in blocks:
                rmsnorm(xt[0:rows, tb, :], rows, gfin, "gfin", xt[0:rows, tb, :], xres(tb), xres(tb))
            P.begin_defer()
            if prompt:
                P.dma("sp", D["yp"][b, tok0:tok0 + 512, :].rearrange("(t p) d -> p t d", p=128), xt,
                      reads=[("x", ti % 2, tb) for tb in range(4)], writes=["y_out"], key=f"x{ti % 2}")
            else:
                P.dma("sp", D["ys"], xt[0:64, 0, :], reads=[("x", ti % 2, 0)], writes=["y_out"], key=f"x{ti % 2}")
            pending_y.extend(P.end_defer())

        load_x(0)
        for tb, rows in tile_geom(tiles[0]):
            norm_block(0, tb, rows, gmix, "gmix")
        for ti in range(len(tiles)):
            run_tile(ti)
        flush_y()

        P.analyze()
        sems = {}
        for en in ENGS:
            sems[("eng", en)] = es.enter_context(nc.semaphore("s_" + en))
        for k in P.dma_keys:
            sems[("dma", k)] = es.enter_context(nc.semaphore("d_" + str(k)))
        block = es.enter_context(nc.Block())
        P.emit(block, sems)
    nc._n_ops = len(P.ops)
    return nc


def _c(a):
    return np.ascontiguousarray(a, dtype=np.float32)


def kernel(x_prompt, x_sample, cache_k, cache_v, state_pool, state_conv,
           norm_mix, w_in, w_a, w_b, pool_w, pool_scale, w_o, norm_ffn,
           w_up, conv_w, conv_b, w_down, norm_final, _tiles=None):
    x_prompt = np.asarray(x_prompt); x_sample = np.asarray(x_sample)
    cache_k = np.asarray(cache_k); cache_v = np.asarray(cache_v)
    state_pool = np.asarray(state_pool); state_conv = np.asarray(state_conv)
    shared = {
        "norm_mix": _c(np.asarray(norm_mix)[0]), "w_in": _c(np.asarray(w_in)[0]), "w_a": _c(np.asarray(w_a)[0]),
        "w_b": _c(np.asarray(w_b)[0]), "pool_w": _c(np.asarray(pool_w)[0]), "pool_scale": _c(np.asarray(pool_scale)[0]),
        "w_o": _c(np.asarray(w_o)[0]), "norm_ffn": _c(np.asarray(norm_ffn)[0]), "w_up": _c(np.asarray(w_up)[0]),
        "conv_w": _c(np.asarray(conv_w)[0]), "conv_b": _c(np.asarray(conv_b)[0]), "w_down": _c(np.asarray(w_down)[0]),
        "norm_final": _c(np.asarray(norm_final)),
    }
    in_maps = []
    for c in range(NCORES):
        m = dict(shared)
        m["xp"] = _c(x_prompt[2 * c:2 * c + 2])
        m["xs"] = _c(x_sample[4 * c:4 * c + 4].reshape(64, 1024))
        m["ck"] = _c(cache_k[0, 4 * c:4 * c + 4].reshape(4, 1024, 512))
        m["cv"] = _c(cache_v[0, 4 * c:4 * c + 4].reshape(4, 1024, 512))
        m["spool"] = _c(state_pool[0, 4 * c:4 * c + 4])
        m["sconv"] = _c(state_conv[0, 4 * c:4 * c + 4])
        in_maps.append(m)
    nc = build_nc(_tiles)
    res = run_bass_kernel_spmd(nc, in_maps, core_ids=list(range(NCORES)))
    R = res.results
    cat = lambda k: np.concatenate([np.asarray(R[c][k]) for c in range(NCORES)], axis=0)
    y_prompt = cat("yp")
    y_sample = cat("ys").reshape(32, 16, 1024)
    k_prompt = cat("kp").reshape(1, 16, 2048, 8, 64)
    v_prompt = cat("vp").reshape(1, 16, 2048, 8, 64)
    pool_prompt = cat("pp").reshape(1, 16, 15, 512)
    conv_prompt = cat("cp").reshape(1, 16, 2, 5632)
    k_sample = cat("ks").reshape(1, 32, 16, 8, 64)
    v_sample = cat("vs").reshape(1, 32, 16, 8, 64)
    pool_sample = cat("ps").reshape(1, 32, 15, 512)
    conv_sample = cat("cs").reshape(1, 32, 2, 5632)
    f = lambda a: np.ascontiguousarray(a, dtype=np.float32)
    return (f(y_prompt), f(y_sample), f(k_prompt), f(v_prompt), f(pool_prompt), f(conv_prompt),
            f(k_sample), f(v_sample), f(pool_sample), f(conv_sample))
```

```python
import os
import contextlib
import numpy as np
import concourse.bass as bass
import concourse.mybir as mybir
from concourse.bass_utils import run_bass_kernel_spmd

F32 = mybir.dt.float32
BF16 = mybir.dt.bfloat16
AF = mybir.ActivationFunctionType
ALU = mybir.AluOpType

ENGS = ("pe", "act", "dve", "pool", "sp")
NSLOT = 6
NCELL = 200
NCORES = 8


class Op:
    __slots__ = ("eng", "fn", "reads", "writes", "dma_key", "idx", "deps", "signaled", "sig_val", "waits")

    def __init__(self, eng, fn, reads, writes, dma_key):
        self.eng = eng
        self.fn = fn
        self.reads = reads
        self.writes = writes
        self.dma_key = dma_key
        self.deps = None
        self.signaled = False
        self.sig_val = None
        self.waits = None


class Prog:
    def __init__(self, same_engine_sync=True):
        self.ops = []
        self.deferred = None
        self.same_engine_sync = same_engine_sync

    def op(self, eng, fn, reads=(), writes=(), dma_key=None):
        psr = [r for r in reads if isinstance(r, tuple) and r and r[0] == "ps"]
        if psr:
            reads = [r for r in reads if not (isinstance(r, tuple) and r and r[0] == "ps")]
            writes = list(writes) + [r for r in psr if r not in writes]
        o = Op(eng, fn, tuple(reads), tuple(writes), dma_key)
        if self.deferred is not None:
            self.deferred.append(o)
        else:
            self.ops.append(o)
        return o

    def begin_defer(self):
        self.deferred = []

    def end_defer(self):
        d, self.deferred = self.deferred, None
        return d

    def push(self, o):
        self.ops.append(o)

    def dma(self, eng, out, in_, reads=(), writes=(), key=None):
        return self.op(eng, lambda e: e.dma_start(out=out, in_=in_), reads, writes, dma_key=key)

    def need_sync(self, dop, o):
        if dop.dma_key is not None:
            return True
        if dop.eng != o.eng:
            return True
        if o.dma_key is not None:
            return True
        if dop.eng in ("pe", "sp"):
            return False
        return self.same_engine_sync

    def analyze(self):
        for i, o in enumerate(self.ops):
            o.idx = i
        last_w = {}
        readers = {}
        for o in self.ops:
            deps = set()
            war = set()
            for r in o.reads:
                w = last_w.get(r)
                if w is not None:
                    deps.add(w)
            for w_ in o.writes:
                w = last_w.get(w_)
                if w is not None:
                    deps.add(w)
                for rd in readers.get(w_, ()):
                    war.add(rd)
            for d in war:
                deps.add(d)
            deps.discard(o.idx)
            o.deps = deps
            for r in o.reads:
                readers.setdefault(r, []).append(o.idx)
            for w_ in o.writes:
                last_w[w_] = o.idx
                readers[w_] = []
        for o in self.ops:
            for d in o.deps:
                dop = self.ops[d]
                if self.need_sync(dop, o):
                    dop.signaled = True
        cnt = {e: 0 for e in ENGS}
        dcnt = {}
        for o in self.ops:
            if o.dma_key is not None:
                dcnt[o.dma_key] = dcnt.get(o.dma_key, 0) + 16
                o.sig_val = dcnt[o.dma_key]
            elif o.signaled:
                cnt[o.eng] += 1
                o.sig_val = cnt[o.eng]
        self.dma_keys = list(dcnt.keys())
        waited = {e: {} for e in ENGS}
        for o in self.ops:
            need = {}
            for d in o.deps:
                dop = self.ops[d]
                if not self.need_sync(dop, o):
                    continue
                sk = ("dma", dop.dma_key) if dop.dma_key is not None else ("eng", dop.eng)
                if dop.sig_val > need.get(sk, 0):
                    need[sk] = dop.sig_val
            ws = []
            wd = waited[o.eng]
            for sk, v in need.items():
                if wd.get(sk, 0) >= v:
                    continue
                wd[sk] = v
                ws.append((sk, v))
            o.waits = ws
        self.final_dma = dict(dcnt)

    def emit(self, block, sems):
        per = {e: [] for e in ENGS}
        for o in self.ops:
            per[o.eng].append(o)
        final_dma = self.final_dma

        def run(eh, ename):
            for o in per[ename]:
                for sk, v in o.waits:
                    eh.wait_ge(sems[sk], v)
                ins = o.fn(eh)
                if o.dma_key is not None:
                    ins.then_inc(sems[("dma", o.dma_key)], 16)
                elif o.signaled:
                    ins.then_inc(sems[("eng", ename)], 1)
            if ename == "sp":
                for k, v in final_dma.items():
                    eh.wait_ge(sems[("dma", k)], v)

        block.tensor(lambda e: run(e, "pe"))
        block.scalar(lambda e: run(e, "act"))
        block.vector(lambda e: run(e, "dve"))
        block.gpsimd(lambda e: run(e, "pool"))
        block.sync(lambda e: run(e, "sp"))


def build_nc(tiles=None):
    nc = bass.Bass("TRN2", target_bir_lowering=False)
    D = {}

    def din(name, shape):
        D[name] = nc.dram_tensor(name, list(shape), F32, kind="ExternalInput").ap()

    def dout(name, shape):
        D[name] = nc.dram_tensor(name, list(shape), F32, kind="ExternalOutput").ap()

    din("xp", (2, 2048, 1024)); din("xs", (64, 1024)); din("ck", (4, 1024, 512)); din("cv", (4, 1024, 512))
    din("spool", (4, 15, 512)); din("sconv", (4, 2, 5632))
    din("norm_mix", (1024,)); din("w_in", (1024, 4096)); din("w_a", (512, 1024)); din("w_b", (512, 1024))
    din("pool_w", (4, 128, 128)); din("pool_scale", (512,)); din("w_o", (1024, 1024)); din("norm_ffn", (1024,))
    din("w_up", (1024, 5632)); din("conv_w", (3, 5632)); din("conv_b", (5632,)); din("w_down", (2816, 1024))
    din("norm_final", (1024,))
    dout("yp", (2, 2048, 1024)); dout("ys", (64, 1024)); dout("kp", (2, 2048, 512)); dout("vp", (2, 2048, 512))
    dout("pp", (2, 15, 512)); dout("cp", (2, 2, 5632)); dout("ks", (64, 512)); dout("vs", (64, 512))
    dout("ps", (4, 15, 512)); dout("cs", (4, 2, 5632))

    P = Prog()
    with contextlib.ExitStack() as es:
        def sb(name, shape, dt):
            return es.enter_context(nc.sbuf_tensor(name, list(shape), dt))

        ring = sb("ring", (128, NSLOT, 4096), BF16)
        xt2 = sb("xt", (128, 2, 4, 1024), F32)
        hT = sb("hT", (128, 8, 512), BF16)
        hb = sb("hb", (128, 1024), BF16)
        KT = sb("KT", (128, 4, 2304), BF16)
        Vb = sb("Vb", (128, 18, 512), BF16)
        attnT = sb("attnT", (128, 4, 512), BF16)
        uctx = sb("uctx", (128, 4, 15), F32)
        ctx = sb("ctx", (128, 4, 2, 44), F32)
        gmix = sb("gmix", (128, 1024), F32)
        gffn = sb("gffn", (128, 1024), F32)
        gfin = sb("gfin", (128, 1024), F32)
        cbf = sb("cbf", (128, 576), BF16)
        identf = sb("identf", (128, 128), F32)
        cpar = sb("cpar", (128, 180), F32)
        cparst = sb("cparst", (44, 4, 128), F32)
        pscst = sb("pscst", (4, 128), F32)
        invc = sb("invc", (128, 4, 15), F32)
        poolw = sb("poolw", (128, 4, 128), BF16)
        small = sb("small", (128, 16), F32)
        QTa = sb("QTa", (128, 4, 512), BF16)
        QTb = sb("QTb", (128, 4, 512), BF16)
        scr = sb("scr", (128, 128 * NCELL), BF16)
        banks = [es.enter_context(nc.psum_tensor(f"bank{i}", [128, 512], F32)) for i in range(8)]
        bankr = [("ps", i) for i in range(8)]

        ident = cbf[:, 0:128]
        negU = cbf[:, 128:256]
        negL = cbf[:, 256:384]
        maskD = cbf[:, 384:512]
        maskS4 = cbf[:, 512:576]

        class Scr:
            def __init__(self):
                self.off = 0

            def reset(self):
                self.off = 0

            def alloc(self, nbytes, dt):
                ncell = (nbytes + 255) // 256
                c0 = self.off
                self.off += ncell
                assert self.off <= NCELL, self.off
                ap = scr[:, c0 * 128: c0 * 128 + nbytes // 2]
                if dt == F32:
                    ap = ap.bitcast(F32)
                return ap, [("scr", c) for c in range(c0, c0 + ncell)]

        SA = Scr()

        def csub(cells, lo, hi):
            return cells[lo // 256:(hi + 255) // 256]

        def mm(out, lhsT, rhs, start, stop, reads, writes, skip=False):
            P.op("pe", lambda e: e.matmul(out, lhsT=lhsT, rhs=rhs, start=start, stop=stop, skip_group_check=skip),
                 reads, writes)

        def tr(out, in_, idn, reads, writes):
            P.op("pe", lambda e: e.transpose(out=out, in_=in_, identity=idn), reads, writes)

        def act(out, in_, func, reads, writes, scale=1.0, bias=None, accum_out=None):
            kw = {}
            if bias is not None:
                kw["bias"] = bias
            if accum_out is not None:
                kw["accum_out"] = accum_out
            P.op("act", lambda e: e.activation(out=out, in_=in_, func=func, scale=scale, **kw), reads, writes)

        def vtt(out, in0, in1, op, reads, writes):
            P.op("dve", lambda e: e.tensor_tensor(out=out, in0=in0, in1=in1, op=op), reads, writes)

        def vstt(out, in0, scalar, in1, op0, op1, reads, writes):
            P.op("dve", lambda e: e.scalar_tensor_tensor(out=out, in0=in0, scalar=scalar, in1=in1, op0=op0, op1=op1),
                 reads, writes)

        def vcopy(out, in_, reads, writes, eng="dve"):
            P.op(eng, lambda e: e.tensor_copy(out=out, in_=in_), reads, writes)

        def memset(ap, val, writes, eng="pool"):
            P.op(eng, lambda e: e.memset(ap, val), (), writes)

        w_in, w_a, w_b, w_o, w_up, w_down = D["w_in"], D["w_a"], D["w_b"], D["w_o"], D["w_up"], D["w_down"]

        def kview(w, c0, n):
            return w[:, c0:c0 + n].rearrange("(k p) n -> p k n", p=128)

        def unit_dmas(kind, arg):
            if kind == "win":
                return [(lambda s: s[:, 0:4096].rearrange("p (k n) -> p k n", k=8), kview(w_in, arg, 512))]
            if kind == "wab":
                return [(lambda s: s[:, 0:2048].rearrange("p (k n) -> p k n", k=4), kview(w_a, arg, 512)),
                        (lambda s: s[:, 2048:4096].rearrange("p (k n) -> p k n", k=4), kview(w_b, arg, 512))]
            if kind == "wo":
                return [(lambda s: s[:, 0:4096].rearrange("p (k n) -> p k n", k=8), kview(w_o, arg, 512))]
            if kind == "up":
                return [(lambda s: s[:, 0:4096].rearrange("p (k t n) -> p k t n", k=8, t=2)[:, :, 0, :],
                         kview(w_up, 256 * arg, 256)),
                        (lambda s: s[:, 0:4096].rearrange("p (k t n) -> p k t n", k=8, t=2)[:, :, 1, :],
                         kview(w_up, 2816 + 256 * arg, 256))]
            if kind == "dn":
                return [(lambda s: s[:, 0:2816].rearrange("p (k n) -> p k n", k=22), kview(w_down, 128 * arg, 128))]
            raise ValueError(kind)

        tile_units = ([("win", 0), ("win", 512), ("win", 1024), ("win", 1536),
                       ("win", 2048), ("win", 3072), ("wab", 0),
                       ("win", 2560), ("win", 3584), ("wab", 512),
                       ("wo", 0), ("wo", 512)]
                      + [("up", i) for i in range(11)] + [("dn", c) for c in range(8)])
        if tiles is None:
            tiles = [("p", 0, 0), ("p", 0, 1), ("p", 0, 2), ("p", 0, 3),
                     ("p", 1, 0), ("p", 1, 1), ("p", 1, 2), ("p", 1, 3), ("s",)]
        all_units = tile_units * len(tiles)
        rstate = {"issued": 0, "cur": 0}

        NU = len(tile_units)
        wscr = nc.dram_tensor("wscr", [NU, 128, 4096], BF16, kind="Internal").ap()

        def unit_used(kind):
            return 2816 if kind == "dn" else 4096

        def ring_writeback(n):
            slot = n % NSLOT
            used = unit_used(all_units[n][0])
            P.dma("pool", wscr[n, :, 0:used], ring[:, slot, 0:used], reads=[("ring", slot)], writes=[("wscr", n)],
                  key=f"wb{slot}")

        def ring_issue():
            n = rstate["issued"]
            if n > 0 and n - 1 < NU and len(all_units) > NU:
                ring_writeback(n - 1)
            if n >= len(all_units):
                return
            slot = n % NSLOT
            sap = ring[:, slot, :]
            if n < NU:
                extra = [("dgethrottle4", n % 3)] + ([("dgethrottle", n % 2)] if all_units[n][0] == "dn" else [])
                for dstb, src in unit_dmas(*all_units[n]):
                    P.dma("pool", dstb(sap), src, writes=[("ring", slot)] + extra, key=f"ring{slot}")
            else:
                u = n % NU
                used = unit_used(all_units[n][0])
                P.dma("pool", ring[:, slot, 0:used], wscr[u, :, 0:used], reads=[("wscr", u)], writes=[("ring", slot)],
                      key=f"ring{slot}")
            rstate["issued"] = n + 1

        def ring_get(kind):
            n = rstate["cur"]
            assert all_units[n][0] == kind, (all_units[n], kind)
            rstate["cur"] = n + 1
            slot = n % NSLOT
            return ring[:, slot, :], ("ring", slot)

        def ring_done(k=1):
            for _ in range(k):
                ring_issue()

        memset(cbf[:], 0.0, ["cbf"])
        memset(identf[:], 0.0, ["identf"])
        memset(small[:, 15:16], 1e-6, ["smalleps"])
        memset(KT[:], 0.0, ["KT"])
        memset(Vb[:], 0.0, ["Vb"])
        memset(QTa[:], 0.0, ["QTa"])
        memset(QTb[:], 0.0, ["QTb"])

        def asel(out, pattern, cmp, fill, cm, res):
            P.op("pool", lambda e: e.affine_select(out=out, in_=out, pattern=pattern, compare_op=cmp, fill=fill,
                                                   base=0, channel_multiplier=cm), [res], [res])

        asel(cbf[:, 0:128], [[-1, 128]], ALU.not_equal, 1.0, 1, "cbf")
        asel(identf[:], [[-1, 128]], ALU.not_equal, 1.0, 1, "identf")
        asel(cbf[:, 128:256], [[1, 128]], ALU.is_ge, -1.0, -1, "cbf")
        asel(cbf[:, 256:384], [[-1, 128]], ALU.is_gt, -1.0, 1, "cbf")
        asel(cbf[:, 384:512], [[1, 128]], ALU.is_gt, -30000.0, -1, "cbf")
        asel(cbf[:, 512:576], [[0, 4], [1, 16]], ALU.is_gt, -30000.0, -1, "cbf")
        P.op("pool", lambda e: e.iota(invc[:], pattern=[[0, 4], [1, 15]], base=1, channel_multiplier=0,
                                      allow_small_or_imprecise_dtypes=True), (), ["invc"])
        for g in range(4):
            P.op("dve", lambda e, g=g: e.tensor_scalar_min(out=invc[:, g, :], in0=invc[:, g, :], scalar1=float(2 << g)),
                 ["invc"], ["invc"])
        P.op("dve", lambda e: e.reciprocal(out=invc[:], in_=invc[:]), ["invc"], ["invc"])
        P.dma("sp", gmix[:], D["norm_mix"].partition_broadcast(128), writes=["gmix"], key="c0")
        P.dma("sp", gffn[:], D["norm_ffn"].partition_broadcast(128), writes=["gffn"], key="c1")
        P.dma("sp", gfin[:], D["norm_final"].partition_broadcast(128), writes=["gfin"], key="c2")
        for j in range(3):
            P.dma("sp", cparst[:, j, :], D["conv_w"][j].rearrange("(k c) -> k c", c=128), writes=["cparst"], key="c3")
        P.dma("sp", cparst[:, 3, :], D["conv_b"].rearrange("(k c) -> k c", c=128), writes=["cparst"], key="c3")
        P.dma("sp", pscst[:], D["pool_scale"].rearrange("(g c) -> g c", c=128), writes=["pscst"], key="c4")
        for j in range(4):
            tr(banks[7][:, j * 44:(j + 1) * 44], cparst[:, j, :], identf[0:44, 0:44], ["cparst", "identf"], [bankr[7]])
        tr(banks[7][:, 176:180], pscst[:], identf[0:4, 0:4], ["pscst", "identf"], [bankr[7]])
        vcopy(cpar[:], banks[7][:, 0:180], [bankr[7]], ["cpar"])
        P.dma("pool", poolw[:], D["pool_w"].rearrange("g c d -> c g d"), writes=["poolw"], key="c5")
        for _ in range(NSLOT):
            ring_issue()

        eps_ap = small[:, 15:16]
        nctr = {"n": 0}

        def rmsnorm(xin, rows, gb, gres, out, xres, outres):
            k = nctr["n"] % 4
            nctr["n"] += 1
            ms_ap, lt_ap, rstd_ap = small[:, 3 * k:3 * k + 1], small[:, 3 * k + 1:3 * k + 2], small[:, 3 * k + 2:3 * k + 3]
            sr = [("small", k)]
            act(hb[0:rows, :], xin, AF.Square, xres, ["hb"] + sr, scale=1.0 / 32.0, accum_out=ms_ap[0:rows])
            act(lt_ap[0:rows], ms_ap[0:rows], AF.Ln, sr + ["smalleps"], sr, bias=eps_ap[0:rows])
            act(rstd_ap[0:rows], lt_ap[0:rows], AF.Exp, sr, sr, scale=-0.5)
            vstt(out, xin, rstd_ap[0:rows], gb[0:rows, :], ALU.mult, ALU.mult, list(xres) + sr + [gres], outres)

        dctr = {"n": 0}
        DBANKS = [7, 0, 1, 2, 3, 4, 5, 6]

        def dbank():
            i = DBANKS[dctr["n"] % len(DBANKS)]
            dctr["n"] += 1
            return banks[i], bankr[i]

        vhoist = {"done": False}
        pending_y = []
        pending_final = []

        def flush_final(k=None):
            n = len(pending_final) if k is None else min(k, len(pending_final))
            for _ in range(n):
                for o_ in pending_final.pop(0):
                    P.push(o_)

        def flush_y():
            flush_final()
            while pending_y:
                P.push(pending_y.pop(0))

        def tile_geom(tile):
            if tile[0] == "p":
                return [(tb, 128) for tb in range(4)]
            return [(0, 64)]

        def load_x(ti):
            tile = tiles[ti]
            xb = xt2[:, ti % 2]
            if tile[0] == "p":
                _, b_, j_ = tile
                P.dma("sp", xb, D["xp"][b_, 512 * j_:512 * j_ + 512, :].rearrange("(t p) d -> p t d", p=128),
                      writes=[("x", ti % 2, tb) for tb in range(4)], key=f"x{ti % 2}")
            else:
                P.dma("sp", xb[0:64, 0, :], D["xs"], writes=[("x", ti % 2, 0)], key=f"x{ti % 2}")

        def norm_block_a(ti, tb, rows, gb, gres):
            xb = xt2[:, ti % 2]
            rmsnorm(xb[0:rows, tb, :], rows, gb, gres, hb[0:rows, :], [("x", ti % 2, tb)], ["hb"])

        def norm_block_b(ti, tb, rows):
            bk, br = dbank()
            pT = bk[:].bitcast(BF16)
            for kc in range(8):
                tr(pT[:, kc * 128: kc * 128 + rows], hb[0:rows, kc * 128:(kc + 1) * 128], ident[0:rows, 0:rows],
                   ["hb", "cbf"], [br])
            src = pT.rearrange("p (k n) -> p k n", k=8)[:, :, 0:rows]
            act(hT[:, :, tb * 128: tb * 128 + rows], src, AF.Copy, [br], [("hT", tb)])

        def norm_block(ti, tb, rows, gb, gres):
            norm_block_a(ti, tb, rows, gb, gres)
            norm_block_b(ti, tb, rows)

        def run_tile(ti):
            tile = tiles[ti]
            xt = xt2[:, ti % 2]
            prompt = tile[0] == "p"
            if prompt:
                _, b, j = tile
                nseq, tl, NT = 1, 512, 512
                blocks = [(tb, 128) for tb in range(4)]
                tok0 = 512 * j
            else:
                nseq, tl, NT = 4, 16, 64
                blocks = [(0, 64)]
                b, j = None, None
            E = 15 + tl
            xres = lambda tb: [("x", ti % 2, tb)]
            hTres = [("hT", tb) for tb, _ in blocks]

            def norm_to_hT(gb, gres):
                for tb, rows in blocks:
                    norm_block(ti, tb, rows, gb, gres)


            SA.reset()
            ebuf = [SA.alloc(2048, F32) for _ in range(2)]
            lnwb = [SA.alloc(1024, BF16) for _ in range(6)]
            lsigb = [SA.alloc(2048, F32) for _ in range(4)]
            tmpb = [SA.alloc(2048, F32) for _ in range(2)]
            Ab = [SA.alloc(1024, BF16) for _ in range(3)]
            if prompt:
                pmixT_ap, pmixr = SA.alloc(4 * NT * 2, BF16)
                pmixT = pmixT_ap.rearrange("p (g n) -> p g n", g=4)
            stage_off = SA.off
            kst = [SA.alloc(2048, F32) for _ in range(2)]
            vst = [SA.alloc(2048, F32) for _ in range(2)]
            b16 = [SA.alloc(1024, BF16) for _ in range(4)]
            cachek = SA.alloc(8192, BF16) if not prompt else None
            ktn = SA.alloc(512, BF16) if not prompt else None
            vnew = SA.alloc(4096, BF16) if not prompt else None
            rot = {"k": 0, "v": 0, "b": 0}

            def proj_tm(slot, sres, tb, rows, col_lo=0):
                bk, br = dbank()
                sv = slot[:, 0:4096].rearrange("p (k n) -> p k n", k=8)
                for kc in range(8):
                    mm(bk[0:rows, :], hT[:, kc, tb * 128 + col_lo: tb * 128 + col_lo + rows], sv[:, kc, :],
                       kc == 0, kc == 7, [("hT", tb), sres], [br])
                return bk, br

            slot, sres = ring_get("win")

            def q_post(st):
                tb, rows, qb, qr = st
                bk2, br2 = dbank()
                pT = bk2[:].bitcast(BF16)
                for m in range(4):
                    tr(pT[:, m * 128: m * 128 + rows], qb[0:rows, m * 128:(m + 1) * 128], ident[0:rows, 0:rows],
                       qr + ["cbf"], [br2])
                src = pT[:, 0:512].rearrange("p (m n) -> p m n", m=4)
                vcopy(QTa[0:64, :, tb * 128: tb * 128 + rows], src[0:64, :, 0:rows], [br2], ["QTa"])
                vcopy(QTb[64:128, :, tb * 128: tb * 128 + rows], src[64:128, :, 0:rows], [br2], ["QTb"])

            prev = None
            for tb, rows in blocks:
                bk, br = proj_tm(slot, sres, tb, rows)
                qb, qr = b16[rot["b"] % 2]; rot["b"] += 1
                act(qb[0:rows, :], bk[0:rows, :], AF.Copy, [br], qr, scale=0.125)
                if prev is not None:
                    q_post(prev)
                flush_final(1)
                prev = (tb, rows, qb, qr)
            ring_done()
            qprev = prev
            slot, sres = ring_get("win")

            def k_post(st):
                tb, rows, kb_, kbr = st
                bk2, br2 = dbank()
                pT = bk2[:].bitcast(BF16)
                for m in range(4):
                    tr(pT[:, m * 128: m * 128 + rows], kb_[0:rows, m * 128:(m + 1) * 128], ident[0:rows, 0:rows],
                       kbr + ["cbf"], [br2])
                src = pT[:, 0:512].rearrange("p (m n) -> p m n", m=4)
                if prompt:
                    c0 = tok0 + tb * 128
                    act(KT[:, :, c0:c0 + 128], src, AF.Copy, [br2], [("KT", (tok0 // 128) + tb)])
                else:
                    act(ktn[0].rearrange("p (m n) -> p m n", m=4), src[:, :, 0:64], AF.Copy, [br2], ktn[1])

            prev = None
            for tb, rows in blocks:
                bk, br = proj_tm(slot, sres, tb, rows)
                if qprev is not None:
                    q_post(qprev)
                    qprev = None
                ks_, ksr = kst[rot["k"] % 2]; rot["k"] += 1
                act(ks_[0:rows, :], bk[0:rows, :], AF.Copy, [br], ksr)
                kb_, kbr = b16[2 + rot["b"] % 2]; rot["b"] += 1
                vcopy(kb_[0:rows, :], ks_[0:rows, :], ksr, kbr)
                if prompt:
                    P.dma("sp", D["kp"][b, tok0 + tb * 128: tok0 + tb * 128 + 128, :], ks_[:, :], reads=ksr,
                          writes=["kp_out"], key=f"kst{(rot['k'] - 1) % 2}")
                else:
                    P.dma("sp", D["ks"], ks_[0:64, :], reads=ksr, writes=["ks_out"], key=f"kst{(rot['k'] - 1) % 2}")
                if prev is not None:
                    k_post(prev)
                prev = (tb, rows, kb_, kbr)
            ring_done()
            kprev = prev
            slot, sres = ring_get("win")
            if prompt:
                for tb, rows in blocks:
                    bk, br = proj_tm(slot, sres, tb, rows)
                    if kprev is not None:
                        k_post(kprev)
                        kprev = None
                    vs_, vsr = vst[rot["v"] % 2]; rot["v"] += 1
                    act(vs_[0:rows, :], bk[0:rows, :], AF.Copy, [br], vsr)
                    blk = (tok0 // 128) + tb
                    vcopy(Vb[:, blk, :], vs_[:, :], vsr, [("V", blk)])
                    P.dma("sp", D["vp"][b, tok0 + tb * 128: tok0 + tb * 128 + 128, :], vs_[:, :], reads=vsr,
                          writes=["vp_out"], key=f"vst{(rot['v'] - 1) % 2}")
            else:
                for s in range(4):
                    bk, br = proj_tm(slot, sres, 0, 16, col_lo=s * 16)
                    if kprev is not None:
                        k_post(kprev)
                        kprev = None
                    vs_, vsr = vst[rot["v"] % 2]; rot["v"] += 1
                    act(vs_[0:16, :], bk[0:16, :], AF.Copy, [br], vsr)
                    vcopy(vnew[0].rearrange("p (s n) -> p s n", s=4)[0:16, s, :], vs_[0:16, :], vsr, vnew[1])
                    P.dma("sp", D["vs"][s * 16:(s + 1) * 16, :], vs_[0:16, :], reads=vsr, writes=["vs_out"],
                          key=f"vst{(rot['v'] - 1) % 2}")
            ring_done()
            flush_y()
            if ti + 1 < len(tiles):
                load_x(ti + 1)

            P4 = {}

            def p4_alloc(AL):
                uT_ap, uTr = AL.alloc(4 * nseq * E * 4, F32)
                sA_ap, sAr = AL.alloc(nseq * E * 4, F32)
                sB_ap, sBr = AL.alloc(nseq * E * 4, F32)
                pooledT_ap, pooledr = AL.alloc(4 * NT * 2, BF16)
                t15_ap, t15r = AL.alloc(64, F32)
                pst_ap, pstr = AL.alloc(2048, F32)
                P4.update(uT=uT_ap.rearrange("p (g s e) -> p g s e", g=4, s=nseq), uTr=uTr,
                          sA=sA_ap.rearrange("p (s e) -> p s e", s=nseq), sAr=sAr,
                          sB=sB_ap.rearrange("p (s e) -> p s e", s=nseq), sBr=sBr,
                          pooledT=pooledT_ap.rearrange("p (g n) -> p g n", g=4), pooledr=pooledr,
                          t15_ap=t15_ap, t15r=t15r, pst_ap=pst_ap, pstr=pstr)

            def u_pe():
                uT, uTr, pst_ap, pstr = P4["uT"], P4["uTr"], P4["pst_ap"], P4["pstr"]
                if prompt:
                    if j == 0:
                        memset(uT[:, :, 0, 0:15], 0.0, uTr, eng="dve")
                    else:
                        vcopy(uT[:, :, 0, 0:15], uctx[:], ["uctx"], uTr)
                else:
                    for s in range(4):
                        P.dma("sp", pst_ap[0:15, :], D["spool"][s], writes=pstr, key="pst")
                        bk, br = dbank()
                        for g in range(4):
                            tr(bk[:, g * 16:(g + 1) * 16], pst_ap[0:16, g * 128:(g + 1) * 128], identf[0:16, 0:16],
                               pstr + ["identf"], [br])
                        vcopy(uT[:, :, s, 0:15], bk[:, 0:64].rearrange("p (g r) -> p g r", g=4)[:, :, 0:15], [br], uTr)
                slot, sres = ring_get("win")
                sv = slot[:, 0:4096].rearrange("p (k n) -> p k n", k=8)
                for g in range(4):
                    bk, br = dbank()
                    for kc in range(8):
                        mm(bk[:, 0:NT], sv[:, kc, g * 128:(g + 1) * 128], hT[:, kc, 0:NT], kc == 0, kc == 7,
                           hTres + [sres], [br])
                    act(uT[:, g, :, 15:E], bk[:, 0:NT].rearrange("p (s t) -> p s t", s=nseq), AF.Copy, [br], uTr)
                ring_done()
                if prompt and j < 3:
                    vcopy(uctx[:], uT[:, :, 0, tl:E], uTr, ["uctx"])
                if (prompt and j == 3) or not prompt:
                    for s in range(nseq):
                        bk, br = dbank()
                        for g in range(4):
                            tr(bk[0:15, g * 128:(g + 1) * 128], uT[:, g, s, tl:E], identf[:], uTr + ["identf"], [br])
                        vcopy(pst_ap[0:15, :], bk[0:15, :], [br], pstr)
                        dst = D["pp"][b] if prompt else D["ps"][s]
                        P.dma("sp", dst, pst_ap[0:15, :], reads=pstr, writes=["pool_out"], key="pst")

            def pool_group(g, fixed_bank=None):
                uT, uTr, sA, sAr, sB, sBr = P4["uT"], P4["uTr"], P4["sA"], P4["sAr"], P4["sB"], P4["sBr"]
                pooledT, pooledr, t15_ap, t15r = P4["pooledT"], P4["pooledr"], P4["t15_ap"], P4["t15r"]
                w = 2 << g
                ug = uT[:, g, :, :]
                vtt(sA[:, :, 1:E], ug[:, :, 1:E], ug[:, :, 0:E - 1], ALU.add, uTr, sAr)
                fin = sA
                finr = sAr
                if g >= 1:
                    vtt(sB[:, :, 3:E], sA[:, :, 3:E], sA[:, :, 1:E - 2], ALU.add, sAr, sBr)
                    fin, finr = sB, sBr
                if g >= 2:
                    vtt(sA[:, :, 7:E], sB[:, :, 7:E], sB[:, :, 3:E - 4], ALU.add, sBr, sAr)
                    fin, finr = sA, sAr
                if g >= 3:
                    vtt(sB[:, :, 15:E], sA[:, :, 15:E], sA[:, :, 7:E - 8], ALU.add, sAr, sBr)
                    fin, finr = sB, sBr
                pg = pooledT[:, g, 0:NT].rearrange("p (s t) -> p s t", s=nseq)
                vstt(pg, fin[:, :, 15:E], 1.0 / w, ug[:, :, 15:E], ALU.mult, ALU.subtract, finr + uTr, pooledr)
                if prompt and j == 0:
                    vtt(t15_ap[:, 0:15], fin[:, 0, 15:30], invc[:, g, :], ALU.mult, finr + ["invc"], t15r)
                    vtt(pooledT[:, g, 0:15], t15_ap[:, 0:15], ug[:, 0, 15:30], ALU.subtract, t15r + uTr, pooledr)
                bk, br = dbank() if fixed_bank is None else (banks[fixed_bank], bankr[fixed_bank])
                mm(bk[:, 0:NT], poolw[:, g, :], pooledT[:, g, 0:NT], True, True, ["poolw"] + pooledr, [br])
                act(pmixT[:, g, 0:NT], bk[:, 0:NT], AF.Identity, [br, "cpar"], pmixr, scale=cpar[:, 176 + g:177 + g])

            ictr = {"n": 0}

            def attention(items, extras=None):
                n = len(items)
                base = ictr["n"]
                ictr["n"] += n

                def bufs(i):
                    g = base + i
                    return (banks[g % 3], bankr[g % 3], ebuf[g % 2], lnwb[g % 6], lsigb[g % 4], tmpb[g % 2], Ab[g % 3])

                def pe_qk(i, it):
                    Z, Zr = bufs(i)[0:2]
                    hp = it["hp"]
                    QT, QTr = (QTa, "QTa") if hp == 0 else (QTb, "QTb")
                    nq = len(it["qk"])
                    for idx, (m, zc0, nn, qc0) in enumerate(it["qk"]):
                        mm(Z[:, zc0:zc0 + nn], KT[:, m, it["kcol"]:it["kcol"] + 128], QT[:, m, qc0:qc0 + nn],
                           idx == 0, (idx == nq - 1) and not it["diag"], [QTr] + it["kres"], [Zr], skip=True)
                    if it["diag"]:
                        mk, mc0, mn = it["mask"]
                        mm(Z[:, mc0:mc0 + mn], ident, mk, False, True, ["cbf"], [Zr], skip=True)

                def act_exp(i, it):
                    Z, Zr, (e_, er) = bufs(i)[0:3]
                    c0, c1 = it["c0"], it["c1"]
                    act(e_[:, c0:c1], Z[:, c0:c1], AF.Exp, [Zr], er)

                def act_ln(i, it):
                    _, _, (e_, er), (ln_, lnr) = bufs(i)[0:4]
                    c0, c1 = it["c0"], it["c1"]
                    act(ln_[:, c0:c1], e_[:, c0:c1], AF.Ln, er, lnr, bias=1.0)

                def dve_lsig(i, it):
                    Z, Zr, _, (ln_, lnr), (ls_, lsr) = bufs(i)[0:5]
                    c0, c1 = it["c0"], it["c1"]
                    vtt(ls_[:, c0:c1], Z[:, c0:c1], ln_[:, c0:c1], ALU.subtract, [Zr] + lnr, lsr)

                def pe_mm1(i, it):
                    (ln_, lnr) = bufs(i)[3]
                    hp = it["hp"]
                    c0, c1 = it["c0"], it["c1"]
                    LB, LBr = banks[3 + hp], bankr[3 + hp]
                    mm(LB[:, c0:c1], negU, ln_[:, c0:c1], it["first"], False, lnr + ["cbf"], [LBr], skip=True)

                def dve_tmp(i, it):
                    _, _, _, _, (ls_, lsr), (t_, tr_), _ = bufs(i)
                    hp = it["hp"]
                    c0, c1 = it["c0"], it["c1"]
                    LB, LBr = banks[3 + hp], bankr[3 + hp]
                    vtt(t_[:, c0:c1], LB[:, c0:c1], ls_[:, c0:c1], ALU.add, [LBr] + lsr, tr_)

                def act_A(i, it):
                    _, _, _, _, _, (t_, tr_), (A_, Ar) = bufs(i)
                    c0, c1 = it["c0"], it["c1"]
                    act(A_[:, c0:c1], t_[:, c0:c1], AF.Exp, tr_, Ar)

                def pe_mm2(i, it):
                    (ln_, lnr) = bufs(i)[3]
                    hp = it["hp"]
                    c0, c1 = it["c0"], it["c1"]
                    LB, LBr = banks[3 + hp], bankr[3 + hp]
                    if not it["last"]:
                        mm(LB[:, c0:c1], negL, ln_[:, c0:c1], False, False, lnr + ["cbf"], [LBr], skip=True)

                def pe_av(i, it):
                    (A_, Ar) = bufs(i)[6]
                    hp = it["hp"]
                    O, Or = banks[5 + hp], bankr[5 + hp]
                    for idx, (m, zc0, nn, qc0) in enumerate(it["qk"]):
                        mm(O[:, zc0:zc0 + nn], Vb[:, it["vblk"], m * 128:(m + 1) * 128], A_[:, zc0:zc0 + nn],
                           it["first"] and idx == 0, it["last"], Ar + it["vres"], [Or], skip=True)
                    if it["last"]:
                        it["evac"](O, Or, hp)

                def at(fn, k):
                    if 0 <= k < n:
                        fn(k, items[k])

                for t in range(n + 6):
                    at(pe_mm2, t - 4)
                    at(pe_av, t - 5)
                    at(pe_mm1, t - 2)
                    at(pe_qk, t)
                    at(act_exp, t - 1)
                    at(act_A, t - 4)
                    at(act_ln, t - 1)
                    at(dve_lsig, t - 2)
                    at(dve_tmp, t - 3)
                    if extras and t >= 3:
                        P.push(extras.pop(0))
                if extras:
                    for o_ in extras:
                        P.push(o_)

            if prompt:
                items = []
                for m in range(4):
                    def evac(O, Or, hp, m=m):
                        act(attnT[hp * 64:(hp + 1) * 64, m, 0:512], O[hp * 64:(hp + 1) * 64, 0:512], AF.Copy,
                            [Or], [("attnT", m, hp)])
                    kmax = 4 * j + 3
                    for kb in range(kmax, -1, -1):
                        for hp in range(2):
                            diag = kb >= 4 * j
                            c0 = 128 * (kb - 4 * j) if diag else 0
                            items.append(dict(hp=hp, c0=c0, c1=512, qk=[(m, c0, 512 - c0, c0)], diag=diag,
                                              mask=(maskD, c0, 128), kcol=kb * 128, vblk=kb,
                                              kres=[("KT", kb)], vres=[("V", kb)],
                                              first=(kb == kmax), last=(kb == 0), evac=evac))
                SA2 = Scr()
                SA2.off = stage_off
                p4_alloc(SA2)
                u_pe()
                P.begin_defer()
                for g in range(4):
                    pool_group(g, fixed_bank=7)
                extras = P.end_defer()
                attention(items, extras)
                if ti + 1 < len(tiles) and tiles[ti + 1][0] == "s":
                    for s in range(2):
                        P.dma("pool", Vb[:, s * 9: s * 9 + 8, :], D["cv"][s].rearrange("(b p) n -> p b n", p=128),
                              writes=[("V", s * 9 + q) for q in range(8)], key=f"vc{s}")
                    vhoist["done"] = True
            else:
                ck_ap, ckr = cachek
                ck3 = ck_ap.rearrange("p (b n) -> p b n", b=8)
                for s in range(4):
                    half = s % 2
                    if not (vhoist["done"] and s < 2):
                        P.dma("pool", Vb[:, half * 9: half * 9 + 8, :], D["cv"][s].rearrange("(b p) n -> p b n", p=128),
                              writes=[("V", half * 9 + q) for q in range(8)], key=f"vc{half}")
                    P.dma("pool", ck3, D["ck"][s].rearrange("(b p) n -> p b n", p=128), writes=ckr, key="ckc")
                    for blk in range(8):
                        bk2, br2 = dbank()
                        pT = bk2[:].bitcast(BF16)
                        for m in range(4):
                            tr(pT[:, m * 128:(m + 1) * 128], ck3[:, blk, m * 128:(m + 1) * 128], ident,
                               ckr + ["cbf"], [br2])
                        c0 = half * 1152 + blk * 128
                        act(KT[:, :, c0:c0 + 128], pT[:, 0:512].rearrange("p (m n) -> p m n", m=4), AF.Copy,
                            [br2], [("KT", half * 9 + blk)])
                    c0n = half * 1152 + 1024
                    act(KT[:, :, c0n:c0n + 16], ktn[0].rearrange("p (m n) -> p m n", m=4)[:, :, s * 16:(s + 1) * 16],
                        AF.Copy, ktn[1], [("KT", half * 9 + 8)])
                    vcopy(Vb[0:16, half * 9 + 8, :], vnew[0].rearrange("p (s n) -> p s n", s=4)[0:16, s, :], vnew[1],
                          [("V", half * 9 + 8)])
                    items = []

                    def evac(O, Or, hp, s=s):
                        act(attnT[hp * 64:(hp + 1) * 64, :, s * 16:(s + 1) * 16],
                            O[hp * 64:(hp + 1) * 64, 0:64].rearrange("p (m n) -> p m n", m=4), AF.Copy,
                            [Or], [("attnT", s, hp)])
                    for kb in range(8, -1, -1):
                        for hp in range(2):
                            items.append(dict(hp=hp, c0=0, c1=64, qk=[(m, 16 * m, 16, s * 16) for m in range(4)],
                                              diag=(kb == 8), mask=(maskS4, 0, 64), kcol=half * 1152 + kb * 128,
                                              vblk=half * 9 + kb, kres=[("KT", half * 9 + kb)],
                                              vres=[("V", half * 9 + kb)],
                                              first=(kb == 8), last=(kb == 0), evac=evac))
                    attention(items)
            attn_res = ([("attnT", m, hp) for m in range(4) for hp in range(2)] if prompt
                        else [("attnT", s, hp) for s in range(4) for hp in range(2)])

            if not prompt:
                SA.reset()
                p4_alloc(SA)
                pmixT_ap, pmixr = SA.alloc(4 * NT * 2, BF16)
                pmixT = pmixT_ap.rearrange("p (g n) -> p g n", g=4)
                u_pe()
                for g in range(4):
                    pool_group(g)
            else:
                SA.reset()
            mergedT_ap, mergedr = SA.alloc(8 * NT * 2, BF16)
            mergedT = mergedT_ap.rearrange("p (k n) -> p k n", k=8)
            sg = [SA.alloc(NT * 4, F32) for _ in range(4)]
            m1b = [SA.alloc(NT * 4, F32) for _ in range(4)]

            for half in range(2):
                gaS, gar = ring_get("win")
                gbS, gbr = ring_get("win")
                abS, abr = ring_get("wab")
                gav = gaS[:, 0:4096].rearrange("p (k n) -> p k n", k=8)
                gbv = gbS[:, 0:4096].rearrange("p (k n) -> p k n", k=8)
                wav = abS[:, 0:2048].rearrange("p (k n) -> p k n", k=4)
                wbv = abS[:, 2048:4096].rearrange("p (k n) -> p k n", k=4)
                for cc in range(4):
                    c = half * 4 + cc
                    cs = slice(cc * 128, (cc + 1) * 128)
                    bk, br = dbank()
                    for kc in range(8):
                        mm(bk[:, 0:NT], gav[:, kc, cs], hT[:, kc, 0:NT], kc == 0, kc == 7, hTres + [gar], [br])
                    (sga, sgar) = sg[2 * (c % 2)]
                    act(sga[:, 0:NT], bk[:, 0:NT], AF.Sigmoid, [br], sgar)
                    bk, br = dbank()
                    for kc in range(4):
                        mm(bk[:, 0:NT], wav[:, kc, cs], attnT[:, kc, 0:NT], kc == 0, kc == 3, attn_res + [abr], [br])
                    (m1, m1r) = m1b[2 * (c % 2)]
                    vtt(m1[:, 0:NT], bk[:, 0:NT], sga[:, 0:NT], ALU.mult, [br] + sgar, m1r)
                    bk, br = dbank()
                    for kc in range(8):
                        mm(bk[:, 0:NT], gbv[:, kc, cs], hT[:, kc, 0:NT], kc == 0, kc == 7, hTres + [gbr], [br])
                    (sgb, sgbr) = sg[2 * (c % 2) + 1]
                    act(sgb[:, 0:NT], bk[:, 0:NT], AF.Sigmoid, [br], sgbr)
                    bk, br = dbank()
                    for kc in range(4):
                        mm(bk[:, 0:NT], wbv[:, kc, cs], pmixT[:, kc, 0:NT], kc == 0, kc == 3, pmixr + [abr], [br])
                    (m2, m2r) = m1b[2 * (c % 2) + 1]
                    vtt(m2[:, 0:NT], bk[:, 0:NT], sgb[:, 0:NT], ALU.mult, [br] + sgbr, m2r)
                    vtt(mergedT[:, c, 0:NT], m1[:, 0:NT], m2[:, 0:NT], ALU.add, m1r + m2r, [("merged", c)])
                ring_done(3)
            mres = [("merged", c) for c in range(8)]
            wos = [ring_get("wo"), ring_get("wo")]
            prevb = None
            for tb, rows in blocks:
                for hh in range(2):
                    slot, sres = wos[hh]
                    sv = slot[:, 0:4096].rearrange("p (k n) -> p k n", k=8)
                    bk, br = dbank()
                    for kc in range(8):
                        mm(bk[0:rows, :], mergedT[:, kc, tb * 128: tb * 128 + rows], sv[:, kc, :], kc == 0, kc == 7,
                           mres + mergedr + [sres], [br])
                    xs_ = xt[0:rows, tb, hh * 512:(hh + 1) * 512]
                    vtt(xs_, bk[0:rows, :], xs_, ALU.add, [br, ("x", ti % 2, tb)], [("x", ti % 2, tb)])
                if prevb is not None:
                    norm_block_b(ti, prevb[0], prevb[1])
                norm_block_a(ti, tb, rows, gffn, "gffn")
                prevb = (tb, rows)
            ring_done(2)
            norm_block_b(ti, prevb[0], prevb[1])

            SA.reset()
            actT_ap, actr = SA.alloc(22 * NT * 2, BF16)
            actT = actT_ap.rearrange("p (k n) -> p k n", k=22)
            accb = [SA.alloc(NT * 4, F32) for _ in range(6)]
            glb = [SA.alloc(NT * 4, F32) for _ in range(2)]
            y2b = [SA.alloc(NT * 4, F32) for _ in range(2)]
            cst_ap, cstr = SA.alloc(1024, F32)
            cst3 = cst_ap.rearrange("p (r c) -> p r c", r=2)
            if prompt and j == 0:
                memset(ctx[:, 0, :, :], 0.0, [("ctx", ch) for ch in range(44)], eng="dve")
            if not prompt:
                for s in range(4):
                    P.dma("sp", cst3[0:44, :, :], D["sconv"][s].rearrange("r (k c) -> k r c", c=128), writes=cstr,
                          key="cst")
                    bk, br = dbank()
                    for r in range(2):
                        tr(bk[:, r * 44:(r + 1) * 44], cst3[0:44, r, :], identf[0:44, 0:44], cstr + ["identf"], [br])
                    vcopy(ctx[:, s, :, :], bk[:, 0:88].rearrange("p (r k) -> p r k", r=2), [br], [("ctx", ch) for ch in range(44)])
            ctxall = [("ctx", ch) for ch in range(44)]
            cctr = {"n": 0}
            bc_ap, bcr = SA.alloc(nseq * 2 * 44 * 4, F32)
            bct_ap, bctr = SA.alloc(44 * 4, F32)
            bc = bc_ap.rearrange("p (s r k) -> p s r k", s=nseq, r=2)
            w0v, w1v, bv = cpar[:, 0:44], cpar[:, 44:88], cpar[:, 132:176]
            for s_ in range(nseq):
                vtt(bc[:, s_, 0, :], ctx[:, s_, 1, :], w1v, ALU.mult, ctxall + ["cpar"], bcr)
                vtt(bct_ap[:, 0:44], ctx[:, s_, 0, :], w0v, ALU.mult, ctxall + ["cpar"], bctr)
                vtt(bc[:, s_, 0, :], bc[:, s_, 0, :], bct_ap[:, 0:44], ALU.add, bcr + bctr, bcr)
                vtt(bc[:, s_, 1, :], ctx[:, s_, 1, :], w0v, ALU.mult, ctxall + ["cpar"], bcr)

            def actres(kc):
                return csub(actr, kc * NT * 2, (kc + 1) * NT * 2)

            def conv_a(bk, br, ch):
                i = cctr["n"] % 6
                cctr["n"] += 1
                acc_ap, accr = accb[i]
                acc = acc_ap[:, 0:NT].rearrange("p (s t) -> p s t", s=nseq)
                src = bk[:, 0:NT].rearrange("p (s t) -> p s t", s=nseq)
                act(acc, src, AF.Identity, [br, "cpar"], accr, scale=cpar[:, 88 + ch:89 + ch],
                    bias=cpar[:, 132 + ch:133 + ch])
                act(ctx[:, 0:nseq, :, ch], src[:, :, tl - 2:tl], AF.Copy, [br], [("ctx", ch)])
                return (src, br, acc, accr, acc_ap, ch)

            def conv_d(st):
                src, br, acc, accr, acc_ap, ch = st
                P.op("pool", lambda e: e.tensor_tensor(out=acc[:, :, 0:2], in0=acc[:, :, 0:2], in1=bc[:, :, :, ch],
                                                       op=ALU.add), accr + bcr, accr)

            def conv_b(st):
                src, br, acc, accr, acc_ap, ch = st
                vstt(acc[:, :, 1:tl], src[:, :, 0:tl - 1], cpar[:, 44 + ch:45 + ch], acc[:, :, 1:tl], ALU.mult, ALU.add,
                     [br, "cpar"] + accr, accr)

            def conv_c(st):
                src, br, acc, accr, acc_ap, ch = st
                vstt(acc[:, :, 2:tl], src[:, :, 0:tl - 2], cpar[:, ch:ch + 1], acc[:, :, 2:tl], ALU.mult, ALU.add,
                     [br, "cpar"] + accr, accr)

            def gelu_mul(pend):
                cg, sts = pend
                (gl, glr) = glb[cg % 2]
                act(gl[:, 0:NT], sts[0][4][:, 0:NT], AF.Gelu_apprx_tanh, sts[0][3], glr)
                P.op("pool", lambda e: e.tensor_tensor(out=actT[:, cg, 0:NT], in0=gl[:, 0:NT], in1=sts[1][4][:, 0:NT],
                                                       op=ALU.mult), glr + sts[1][3], actres(cg))

            pending = None
            for i in range(11):
                slot, sres = ring_get("up")
                sv = slot[:, 0:4096].rearrange("p (k t n) -> p k t n", k=8, t=2)
                for a in range(2):
                    cg = 2 * i + a
                    sts = []
                    for t in range(2):
                        bk, br = dbank()
                        for kc in range(8):
                            mm(bk[:, 0:NT], sv[:, kc, t, a * 128:(a + 1) * 128], hT[:, kc, 0:NT], kc == 0, kc == 7,
                               hTres + [sres], [br])
                        sts.append(conv_a(bk, br, cg + 22 * t))
                    conv_b(sts[0]); conv_b(sts[1])
                    if pending is not None:
                        gelu_mul(pending)
                    conv_c(sts[0]); conv_c(sts[1])
                    conv_d(sts[0]); conv_d(sts[1])
                    pending = (cg, sts)
                ring_done()
            gelu_mul(pending)
            if (prompt and j == 3) or not prompt:
                for s in range(nseq):
                    bk, br = dbank()
                    for r in range(2):
                        tr(bk[0:44, r * 128:(r + 1) * 128], ctx[:, s, r, :], identf[:], [("ctx", ch) for ch in range(44)] + ["identf"], [br])
                    vcopy(cst3[0:44, :, :], bk[0:44, 0:256].rearrange("p (r c) -> p r c", r=2), [br], cstr)
                    dst = D["cp"][b] if prompt else D["cs"][s]
                    P.dma("sp", dst.rearrange("r (k c) -> k r c", c=128), cst3[0:44, :, :], reads=cstr,
                          writes=["conv_out"], key="cst")
            nblocks = tile_geom(tiles[ti + 1]) if ti + 1 < len(tiles) else []

            def dn_post(st):
                c, y2, y2r = st
                bk2, br2 = dbank()
                for tb, rows in blocks:
                    tr(bk2[0:rows, tb * 128:(tb + 1) * 128], y2[:, tb * 128: tb * 128 + rows], identf[:],
                       y2r + ["identf"], [br2])
                if prompt:
                    xv = xt[:, :, c * 128:(c + 1) * 128]
                    vtt(xv, bk2[:, :].rearrange("p (t n) -> p t n", t=4), xv, ALU.add,
                        [br2] + [("x", ti % 2, tb) for tb in range(4)], [("x", ti % 2, tb) for tb in range(4)])
                else:
                    xv = xt[0:64, 0, c * 128:(c + 1) * 128]
                    vtt(xv, bk2[0:64, 0:128], xv, ALU.add, [br2, ("x", ti % 2, 0)], [("x", ti % 2, 0)])

            prev = None
            for c in range(8):
                slot, sres = ring_get("dn")
                sv = slot[:, 0:2816].rearrange("p (k n) -> p k n", k=22)
                bk, br = dbank()
                for kc in range(22):
                    mm(bk[:, 0:NT], sv[:, kc, :], actT[:, kc, 0:NT], kc == 0, kc == 21, actres(kc) + [sres], [br])
                (y2, y2r) = y2b[c % 2]
                act(y2[:, 0:NT], bk[:, 0:NT], AF.Copy, [br], y2r)
                if prev is not None:
                    dn_post(prev)
                if 1 <= c <= len(nblocks):
                    norm_block_b(ti + 1, nblocks[c - 1][0], nblocks[c - 1][1])
                if c < len(nblocks):
                    norm_block_a(ti + 1, nblocks[c][0], nblocks[c][1], gmix, "gmix")
                prev = (c, y2, y2r)
                ring_done()
            dn_post(prev)

            for tb, rows in blocks:
                P.begin_defer()
                rmsnorm(xt[0:rows, tb, :], rows, gfin, "gfin", xt[0:rows, tb, :], xres(tb), xres(tb))
                pending_final.append(P.end_defer())
            P.begin_defer()
            if prompt:
                P.dma("sp", D["yp"][b, tok0:tok0 + 512, :].rearrange("(t p) d -> p t d", p=128), xt,
                      reads=[("x", ti % 2, tb) for tb in range(4)], writes=["y_out"], key=f"x{ti % 2}")
            else:
                P.dma("sp", D["ys"], xt[0:64, 0, :], reads=[("x", ti % 2, 0)], writes=["y_out"], key=f"x{ti % 2}")
            pending_y.extend(P.end_defer())

        load_x(0)
        for tb, rows in tile_geom(tiles[0]):
            norm_block(0, tb, rows, gmix, "gmix")
        for ti in range(len(tiles)):
            run_tile(ti)
        flush_y()

        P.analyze()
        sems = {}
        for en in ENGS:
            sems[("eng", en)] = es.enter_context(nc.semaphore("s_" + en))
        for k in P.dma_keys:
            sems[("dma", k)] = es.enter_context(nc.semaphore("d_" + str(k)))
        block = es.enter_context(nc.Block())
        P.emit(block, sems)
    nc._n_ops = len(P.ops)
    return nc


def _c(a):
    return np.ascontiguousarray(a, dtype=np.float32)


def kernel(x_prompt, x_sample, cache_k, cache_v, state_pool, state_conv,
           norm_mix, w_in, w_a, w_b, pool_w, pool_scale, w_o, norm_ffn,
           w_up, conv_w, conv_b, w_down, norm_final, _tiles=None):
    x_prompt = np.asarray(x_prompt); x_sample = np.asarray(x_sample)
    cache_k = np.asarray(cache_k); cache_v = np.asarray(cache_v)
    state_pool = np.asarray(state_pool); state_conv = np.asarray(state_conv)
    shared = {
        "norm_mix": _c(np.asarray(norm_mix)[0]), "w_in": _c(np.asarray(w_in)[0]), "w_a": _c(np.asarray(w_a)[0]),
        "w_b": _c(np.asarray(w_b)[0]), "pool_w": _c(np.asarray(pool_w)[0]), "pool_scale": _c(np.asarray(pool_scale)[0]),
        "w_o": _c(np.asarray(w_o)[0]), "norm_ffn": _c(np.asarray(norm_ffn)[0]), "w_up": _c(np.asarray(w_up)[0]),
        "conv_w": _c(np.asarray(conv_w)[0]), "conv_b": _c(np.asarray(conv_b)[0]), "w_down": _c(np.asarray(w_down)[0]),
        "norm_final": _c(np.asarray(norm_final)),
    }
    in_maps = []
    for c in range(NCORES):
        m = dict(shared)
        m["xp"] = _c(x_prompt[2 * c:2 * c + 2])
        m["xs"] = _c(x_sample[4 * c:4 * c + 4].reshape(64, 1024))
        m["ck"] = _c(cache_k[0, 4 * c:4 * c + 4].reshape(4, 1024, 512))
        m["cv"] = _c(cache_v[0, 4 * c:4 * c + 4].reshape(4, 1024, 512))
        m["spool"] = _c(state_pool[0, 4 * c:4 * c + 4])
        m["sconv"] = _c(state_conv[0, 4 * c:4 * c + 4])
        in_maps.append(m)
    nc = build_nc(_tiles)
    res = run_bass_kernel_spmd(nc, in_maps, core_ids=list(range(NCORES)))
    R = res.results
    cat = lambda k: np.concatenate([np.asarray(R[c][k]) for c in range(NCORES)], axis=0)
    y_prompt = cat("yp")
    y_sample = cat("ys").reshape(32, 16, 1024)
    k_prompt = cat("kp").reshape(1, 16, 2048, 8, 64)
    v_prompt = cat("vp").reshape(1, 16, 2048, 8, 64)
    pool_prompt = cat("pp").reshape(1, 16, 15, 512)
    conv_prompt = cat("cp").reshape(1, 16, 2, 5632)
    k_sample = cat("ks").reshape(1, 32, 16, 8, 64)
    v_sample = cat("vs").reshape(1, 32, 16, 8, 64)
    pool_sample = cat("ps").reshape(1, 32, 15, 512)
    conv_sample = cat("cs").reshape(1, 32, 2, 5632)
    f = lambda a: np.ascontiguousarray(a, dtype=np.float32)
    return (f(y_prompt), f(y_sample), f(k_prompt), f(v_prompt), f(pool_prompt), f(conv_prompt),
            f(k_sample), f(v_sample), f(pool_sample), f(conv_sample))
```
